# Optimizing a Trainium2 kernel written in Bass

```python
import jax, jax.numpy as jnp
from jax import lax
import numpy as np

D_MODEL = 1024
BATCH = 4
SEQ = 8192
DEPTH = 4

CHUNK = 64
N_EVEN = (DEPTH + 1) // 2
N_ODD = DEPTH // 2
EPS = 1e-6

LRU_WIDTH = D_MODEL // 2
LRU_HEADS = 8
LRU_HEAD_DIM = LRU_WIDTH // LRU_HEADS
LRU_CONV = 4
LRU_C = 8.0
SCONV_WIDTH = D_MODEL // 2
SCONV_K = 3
EVEN_IN = 2 * LRU_WIDTH + 3 * SCONV_WIDTH
SPLIT_EVEN = (LRU_WIDTH, 2 * LRU_WIDTH, 2 * LRU_WIDTH + SCONV_WIDTH, 2 * LRU_WIDTH + 2 * SCONV_WIDTH)
EVEN_MIX = LRU_WIDTH + SCONV_WIDTH

POOL_WIDTH = D_MODEL // 2
POOL_WINDOWS = (2, 4, 8, 16)
POOL_GROUPS = len(POOL_WINDOWS)
POOL_GROUP = POOL_WIDTH // POOL_GROUPS
HGRN_WIDTH = D_MODEL // 2
HGRN_HEADS = 4
HGRN_DK = HGRN_WIDTH // HGRN_HEADS
HGRN_DV = HGRN_WIDTH // HGRN_HEADS
ODD_IN = POOL_WIDTH + 4 * HGRN_WIDTH
SPLIT_ODD = (POOL_WIDTH, POOL_WIDTH + HGRN_WIDTH, POOL_WIDTH + 2 * HGRN_WIDTH, POOL_WIDTH + 3 * HGRN_WIDTH)
ODD_MIX = POOL_WIDTH + HGRN_WIDTH

FFN_DIM = 256 * ((8 * D_MODEL // 3 + 255) // 256)
FFN_K = 3

kernel_name = "hybrid_rglru_shortconv_pool_hgrn2_convffn"


def rmsnorm(x, g):
    xf = x.astype(jnp.float32)
    y = xf * lax.rsqrt(jnp.mean(xf * xf, axis=-1, keepdims=True) + EPS)
    return (y * g.astype(jnp.float32)).astype(x.dtype)


def causal_dwconv(u, w):
    K = w.shape[0]
    S = u.shape[1]
    up = jnp.pad(u, ((0, 0), (K - 1, 0), (0, 0)))
    y = up[:, K - 1:K - 1 + S] * w[K - 1]
    for j in range(K - 1):
        y = y + up[:, j:j + S] * w[j]
    return y


def rg_lru(xc, w_a, b_a, w_i, b_i, lam):
    f32 = jnp.float32
    Bsz, S, _ = xc.shape
    xf = xc.astype(f32)
    xh = xf.reshape(Bsz, S, LRU_HEADS, LRU_HEAD_DIM)
    r = jax.nn.sigmoid(jnp.einsum('bshi,hij->bshj', xh, w_a.astype(f32)).reshape(Bsz, S, LRU_WIDTH) + b_a.astype(f32))
    ig = jax.nn.sigmoid(jnp.einsum('bshi,hij->bshj', xh, w_i.astype(f32)).reshape(Bsz, S, LRU_WIDTH) + b_i.astype(f32))
    log_a = -LRU_C * r * jax.nn.softplus(-lam.astype(f32))
    mult = jnp.sqrt(-jnp.expm1(2.0 * log_a))
    mult = jnp.where(jnp.arange(S)[None, :, None] == 0, 1.0, mult)
    hb = mult * ig * xf

    def combine(left, right):
        a_l, h_l = left
        a_r, h_r = right
        return a_l * a_r, a_r * h_l + h_r

    _, h = lax.associative_scan(combine, (jnp.exp(log_a), hb), axis=1)
    return h


def pool_mixer(u, w_grp, scale):
    f32 = jnp.float32
    Bsz, S, _ = u.shape
    uf = u.astype(f32)
    cs = jnp.cumsum(uf, axis=1)
    t1 = jnp.arange(1, S + 1, dtype=f32)[None, :, None]
    outs = []
    for gi, win in enumerate(POOL_WINDOWS):
        sl = slice(gi * POOL_GROUP, (gi + 1) * POOL_GROUP)
        c = cs[..., sl]
        lag = jnp.pad(c, ((0, 0), (win, 0), (0, 0)))[:, :S]
        outs.append((c - lag) / jnp.minimum(t1, float(win)) - uf[..., sl])
    p = jnp.stack(outs, axis=2)
    y = jnp.einsum('bsgi,gio->bsgo', p, w_grp.astype(f32)).reshape(Bsz, S, POOL_WIDTH)
    return (y * scale.astype(f32)).astype(u.dtype)


def hgrn2(q, fz, v, g, lb, norm_g):
    f32 = jnp.float32
    Bsz, S, _ = q.shape
    nC = S // CHUNK
    fz = fz.astype(f32)
    lb = lb.astype(f32)
    log_f = jnp.logaddexp(jnp.log(lb), jnp.log1p(-lb) + jax.nn.log_sigmoid(fz))
    k = (1.0 - lb) * jax.nn.sigmoid(-fz)

    def chunked(a, d):
        return a.astype(f32).reshape(Bsz, nC, CHUNK, HGRN_HEADS, d)

    qc = chunked(q, HGRN_DK)
    kc = chunked(k, HGRN_DK)
    lfc = chunked(log_f, HGRN_DK)
    vc = chunked(v, HGRN_DV)

    def intra_step(s, inp):
        lf_t, k_t, v_t, q_t = inp
        s = jnp.exp(lf_t)[..., None] * s + k_t[..., None] * v_t[..., None, :]
        return s, jnp.einsum('bchk,bchkv->bchv', q_t, s)

    tm = lambda a: jnp.moveaxis(a, 2, 0)
    s0 = jnp.zeros((Bsz, nC, HGRN_HEADS, HGRN_DK, HGRN_DV), f32)
    ds, o_intra = lax.scan(intra_step, s0, (tm(lfc), tm(kc), tm(vc), tm(qc)))
    o_intra = jnp.moveaxis(o_intra, 0, 2)

    G = jnp.cumsum(lfc, axis=2)

    def inter_step(s, inp):
        dec, ds_c = inp
        return jnp.exp(dec)[..., None] * s + ds_c, s

    _, s_prev = lax.scan(inter_step, jnp.zeros((Bsz, HGRN_HEADS, HGRN_DK, HGRN_DV), f32),
                         (jnp.moveaxis(G[:, :, -1], 1, 0), jnp.moveaxis(ds, 1, 0)))
    o_inter = jnp.einsum('bclhk,cbhkv->bclhv', qc * jnp.exp(G), s_prev)
    o = (o_intra + o_inter).reshape(Bsz, S, HGRN_HEADS, HGRN_DV)
    o = o * lax.rsqrt(jnp.mean(o * o, axis=-1, keepdims=True) + EPS) * norm_g.astype(f32).reshape(HGRN_HEADS, HGRN_DV)
    o = o.reshape(Bsz, S, HGRN_WIDTH) * jax.nn.silu(g.astype(f32))
    return o.astype(q.dtype)


def setup_inputs(seed: int = 0) -> dict:
    key = jax.random.key(seed)
    ks = iter(jax.random.split(key, 32))
    f32 = jnp.float32

    def nrm(shape, s):
        return jax.random.normal(next(ks), shape, f32) * s

    x = nrm((BATCH, SEQ, D_MODEL), 1.0)
    g_mix = 1.0 + nrm((DEPTH, D_MODEL), 0.02)
    g_ffn = 1.0 + nrm((DEPTH, D_MODEL), 0.02)
    g_final = 1.0 + nrm((D_MODEL,), 0.02)
    w_in_even = nrm((N_EVEN, D_MODEL, EVEN_IN), D_MODEL ** -0.5)
    w_out_even = nrm((N_EVEN, EVEN_MIX, D_MODEL), EVEN_MIX ** -0.5)
    lru_conv_w = nrm((N_EVEN, LRU_CONV, LRU_WIDTH), LRU_CONV ** -0.5)
    lru_conv_b = nrm((N_EVEN, LRU_WIDTH), 0.02)
    lru_wa = nrm((N_EVEN, LRU_HEADS, LRU_HEAD_DIM, LRU_HEAD_DIM), LRU_HEAD_DIM ** -0.5)
    lru_ba = nrm((N_EVEN, LRU_WIDTH), 0.02)
    lru_wi = nrm((N_EVEN, LRU_HEADS, LRU_HEAD_DIM, LRU_HEAD_DIM), LRU_HEAD_DIM ** -0.5)
    lru_bi = nrm((N_EVEN, LRU_WIDTH), 0.02)
    a_c = jax.random.uniform(next(ks), (N_EVEN, LRU_WIDTH), f32, minval=0.9, maxval=0.999)
    s_a = a_c ** (1.0 / LRU_C)
    lru_lambda = jnp.log(s_a) - jnp.log1p(-s_a)
    sconv_w = nrm((N_EVEN, SCONV_K, SCONV_WIDTH), SCONV_K ** -0.5)
    w_in_odd = nrm((N_ODD, D_MODEL, ODD_IN), D_MODEL ** -0.5)
    w_out_odd = nrm((N_ODD, ODD_MIX, D_MODEL), ODD_MIX ** -0.5)
    pool_w = nrm((N_ODD, POOL_GROUPS, POOL_GROUP, POOL_GROUP), POOL_GROUP ** -0.5)
    pool_scale = 1.0 + nrm((N_ODD, POOL_WIDTH), 0.02)
    hgrn_lb_logits = nrm((N_ODD, HGRN_WIDTH), 0.5)
    hgrn_norm_g = 1.0 + nrm((N_ODD, HGRN_WIDTH), 0.02)
    ffn_w_up = nrm((DEPTH, D_MODEL, FFN_DIM), D_MODEL ** -0.5)
    ffn_w_gate = nrm((DEPTH, D_MODEL, FFN_DIM), D_MODEL ** -0.5)
    ffn_conv_w = nrm((DEPTH, FFN_K, FFN_DIM), FFN_K ** -0.5)
    ffn_conv_b = nrm((DEPTH, FFN_DIM), 0.02)
    ffn_w_down = nrm((DEPTH, FFN_DIM, D_MODEL), FFN_DIM ** -0.5)
    return {"x": x, "g_mix": g_mix, "g_ffn": g_ffn, "g_final": g_final,
            "w_in_even": w_in_even, "w_out_even": w_out_even, "lru_conv_w": lru_conv_w, "lru_conv_b": lru_conv_b,
            "lru_wa": lru_wa, "lru_ba": lru_ba, "lru_wi": lru_wi, "lru_bi": lru_bi, "lru_lambda": lru_lambda,
            "sconv_w": sconv_w, "w_in_odd": w_in_odd, "w_out_odd": w_out_odd, "pool_w": pool_w,
            "pool_scale": pool_scale, "hgrn_lb_logits": hgrn_lb_logits, "hgrn_norm_g": hgrn_norm_g,
            "ffn_w_up": ffn_w_up, "ffn_w_gate": ffn_w_gate, "ffn_conv_w": ffn_conv_w, "ffn_conv_b": ffn_conv_b,
            "ffn_w_down": ffn_w_down}


def reference(x, g_mix, g_ffn, g_final, w_in_even, w_out_even, lru_conv_w, lru_conv_b, lru_wa, lru_ba,
              lru_wi, lru_bi, lru_lambda, sconv_w, w_in_odd, w_out_odd, pool_w, pool_scale,
              hgrn_lb_logits, hgrn_norm_g, ffn_w_up, ffn_w_gate, ffn_conv_w, ffn_conv_b, ffn_w_down):
    f32 = jnp.float32
    lb_all = jnp.cumsum(jax.nn.softmax(hgrn_lb_logits.astype(f32), axis=0), axis=0)
    lb_all = lb_all - lb_all[0]
    for l in range(DEPTH):
        h = rmsnorm(x, g_mix[l])
        if l % 2 == 0:
            e = l // 2
            z = h @ w_in_even[e]
            xa, ga, hb, bg, cg = jnp.split(z, SPLIT_EVEN, axis=-1)
            xa = causal_dwconv(xa, lru_conv_w[e]) + lru_conv_b[e]
            ya = (rg_lru(xa, lru_wa[e], lru_ba[e], lru_wi[e], lru_bi[e], lru_lambda[e])
                  * jax.nn.gelu(ga.astype(f32))).astype(x.dtype)
            yb = bg * causal_dwconv(cg * hb, sconv_w[e])
            x = x + jnp.concatenate([ya, yb], axis=-1) @ w_out_even[e]
        else:
            o = l // 2
            z = h @ w_in_odd[o]
            uc, q, fz, iv, gd = jnp.split(z, SPLIT_ODD, axis=-1)
            yc = pool_mixer(uc, pool_w[o], pool_scale[o])
            yd = hgrn2(q, fz, iv, gd, lb_all[o], hgrn_norm_g[o])
            x = x + jnp.concatenate([yc, yd], axis=-1) @ w_out_odd[o]
        h = rmsnorm(x, g_ffn[l])
        u = causal_dwconv(h @ ffn_w_up[l], ffn_conv_w[l]) + ffn_conv_b[l]
        x = x + (jax.nn.gelu(u) * (h @ ffn_w_gate[l])) @ ffn_w_down[l]
    return rmsnorm(x, g_final)
```

```python
from contextlib import ExitStack
import numpy as np
import concourse.bass as bass
import concourse.mybir as mybir
from concourse.bass_utils import run_bass_kernel_spmd

F32 = mybir.dt.float32
BF16 = mybir.dt.bfloat16
AF = mybir.ActivationFunctionType
ALU = mybir.AluOpType

D = 1024
KC = 8
FF = 2816
FC = 22
EPS = 1e-6
DEPTH = 4
SEQ = 8192
BATCH = 4
POOL_WINS = (2, 4, 8, 16)


def vec_layout():
    off = {}
    n = 0

    def add(name, c):
        nonlocal n
        off[name] = n
        n += c

    for l in range(DEPTH):
        add(f"gmix{l}", 8)
        add(f"gffn{l}", 8)
        for j in range(3):
            add(f"fcw{l}_{j}", FC)
        add(f"fcb{l}", FC)
    for e in range(2):
        for j in range(4):
            add(f"lcw{e}_{j}", 4)
        add(f"lcb{e}", 4)
        add(f"lba{e}", 4)
        add(f"lbi{e}", 4)
        add(f"lam{e}", 4)
        for j in range(3):
            add(f"scw{e}_{j}", 4)
    for o in range(2):
        add(f"psc{o}", 4)
        add(f"lbl{o}", 4)
        add(f"hng{o}", 4)
    return off, n


VOFF, NV = vec_layout()


def _cols(v):
    v = np.asarray(v, np.float32).reshape(-1, 128)
    return v.T


def host_pack(inp):
    vecs = np.zeros((128, NV), np.float32)

    def put(name, v):
        c = _cols(v)
        vecs[:, VOFF[name]:VOFF[name] + c.shape[1]] = c

    for l in range(DEPTH):
        put(f"gmix{l}", inp["g_mix"][l])
        put(f"gffn{l}", inp["g_ffn"][l])
        for j in range(3):
            put(f"fcw{l}_{j}", inp["ffn_conv_w"][l, j])
        put(f"fcb{l}", inp["ffn_conv_b"][l])
    for e in range(2):
        for j in range(4):
            put(f"lcw{e}_{j}", inp["lru_conv_w"][e, j])
        put(f"lcb{e}", inp["lru_conv_b"][e])
        put(f"lba{e}", inp["lru_ba"][e])
        put(f"lbi{e}", inp["lru_bi"][e])
        put(f"lam{e}", inp["lru_lambda"][e])
        for j in range(3):
            put(f"scw{e}_{j}", inp["sconv_w"][e, j])
    for o in range(2):
        put(f"psc{o}", inp["pool_scale"][o])
        put(f"lbl{o}", inp["hgrn_lb_logits"][o])
        put(f"hng{o}", inp["hgrn_norm_g"][o])
    wbd = np.zeros((2, 2, 4, 128, 128), np.float32)
    for e in range(2):
        for gi, nm in enumerate(("lru_wa", "lru_wi")):
            w = np.asarray(inp[nm][e], np.float32)
            for c in range(4):
                wbd[e, gi, c, 0:64, 0:64] = w[2 * c]
                wbd[e, gi, c, 64:128, 64:128] = w[2 * c + 1]
    gfin = np.ascontiguousarray(np.broadcast_to(np.asarray(inp["g_final"], np.float32)[None, :], (128, D)))
    ident = np.eye(128, dtype=np.float32)
    s = np.arange(128)[:, None]
    t = np.arange(128)[None, :]
    trimask = ((s <= t) & (s // 64 == t // 64)).astype(np.float32)
    cmask = np.ones((128, 512), np.float32)
    cmask[:, ::64] = 0.0
    poolinv = np.zeros((128, 4, 16), np.float32)
    for g, win in enumerate(POOL_WINS):
        poolinv[:, g, :] = 1.0 / np.minimum(np.arange(1, 17), win)[None, :]
    ones128 = np.full((128, 128), 1.0 / 128, np.float32)
    consts = np.concatenate([ident, trimask, cmask, poolinv.reshape(128, 64), ones128], axis=1)
    f = lambda k: np.ascontiguousarray(np.asarray(inp[k], np.float32))
    return {
        "vecs": vecs, "wbd": wbd, "gfin": gfin, "consts": np.ascontiguousarray(consts),
        "w_in_even": f("w_in_even"), "w_out_even": f("w_out_even"), "w_in_odd": f("w_in_odd"),
        "w_out_odd": f("w_out_odd"), "pool_w": f("pool_w"), "ffn_w_up": f("ffn_w_up"),
        "ffn_w_gate": f("ffn_w_gate"), "ffn_w_down": f("ffn_w_down"),
    }


NCONST = 128 + 128 + 512 + 64 + 128


class Tok:
    __slots__ = ("sem", "val")

    def __init__(self, sem, val):
        self.sem = sem
        self.val = val


class KeyState:
    __slots__ = ("w", "r")

    def __init__(self):
        self.w = {}
        self.r = {}


def _merge(d, tok):
    k = id(tok.sem)
    if k not in d or d[k].val < tok.val:
        d[k] = tok


class Builder:
    def __init__(self, nc, stack):
        self.nc = nc
        self.stack = stack
        self.eng = {"pe": nc.tensor, "dve": nc.vector, "act": nc.scalar, "pool": nc.gpsimd, "sp": nc.sync}
        self.sem = {}
        self.cnt = {}
        self.waited = {k: {} for k in self.eng}
        for k in self.eng:
            self.sem[k] = stack.enter_context(nc.semaphore("s_" + k))
            self.cnt[k] = 0
        self.dsems = {}
        self.dcnt = {}
        self.keys = {}
        self.ninstr = 0
        self.dead = False

    def stop_at(self, n):
        if DBG_STOP == n:
            self.dead = True

    def ks(self, k):
        s = self.keys.get(k)
        if s is None:
            s = self.keys[k] = KeyState()
        return s

    def wait(self, e, toks):
        if self.dead:
            return
        h = self.eng[e]
        for t in toks:
            k = id(t.sem)
            if self.waited[e].get(k, 0) >= t.val:
                continue
            self.waited[e][k] = t.val
            h.wait_ge(t.sem, t.val)
            self.ninstr += 1

    def _deps(self, e, r, w):
        own = id(self.sem[e]) if e in self.sem else None
        best = {}
        for k in r:
            for t in self.ks(k).w.values():
                if e == "pe" and id(t.sem) == own:
                    continue
                _merge(best, t)
        for k in w:
            st = self.ks(k)
            for t in list(st.w.values()) + list(st.r.values()):
                if id(t.sem) == own:
                    continue
                _merge(best, t)
        return list(best.values())

    def _record(self, tok, r, w):
        for k in r:
            _merge(self.ks(k).r, tok)
        for k in w:
            st = self.ks(k)
            st.w = {id(tok.sem): tok}
            st.r = {}

    def op(self, e, fn, r=(), w=(), inc=True):
        if self.dead:
            return None
        self.wait(e, self._deps(e, r, w))
        ins = fn(self.eng[e])
        self.ninstr += 1
        if inc:
            self.cnt[e] += 1
            ins.then_inc(self.sem[e], 1)
            tok = Tok(self.sem[e], self.cnt[e])
        else:
            tok = Tok(self.sem[e], self.cnt[e] + 1)
        self._record(tok, r, w)
        return tok

    def dma(self, q, slot, out, in_, r=(), w=()):
        if self.dead:
            return None
        if slot not in self.dsems:
            self.dsems[slot] = self.stack.enter_context(self.nc.semaphore("d_" + slot))
            self.dcnt[slot] = 0
        self.wait(q, self._deps("dma", r, w))
        ins = self.eng[q].dma_start(out=out, in_=in_)
        self.ninstr += 1
        self.dcnt[slot] += 16
        ins.then_inc(self.dsems[slot], 16)
        tok = Tok(self.dsems[slot], self.dcnt[slot])
        self._record(tok, r, w)
        return tok

    def mm_gen(self, out_ap, out_key, pairs, r, chunk, pair_r=None):
        n = len(pairs)
        if not self.dead:
            self.wait("pe", self._deps("pe", r, [out_key]))
        for i, (l, rh) in enumerate(pairs):
            last = (i == n - 1)
            if not self.dead:
                if pair_r is not None:
                    self.wait("pe", self._deps("pe", [pair_r[i]], ()))
                ins = self.eng["pe"].matmul(out_ap, l, rh, start=(i == 0), stop=last)
                self.ninstr += 1
                if last:
                    self.cnt["pe"] += 1
                    ins.then_inc(self.sem["pe"], 1)
                    self._record(Tok(self.sem["pe"], self.cnt["pe"]), list(r) + (list(pair_r) if pair_r is not None else []), [out_key])
            if (i + 1) % chunk == 0 and not last:
                yield

    def mm(self, out_ap, out_key, pairs, r):
        for _ in self.mm_gen(out_ap, out_key, pairs, r, 1 << 30):
            pass

    def barrier(self, force=False):
        if self.dead and not force:
            return
        self.dead = False
        toks = [Tok(self.sem[e], self.cnt[e]) for e in self.eng if self.cnt[e] > 0]
        toks += [Tok(self.dsems[s], self.dcnt[s]) for s in self.dsems if self.dcnt[s] > 0]
        for e in self.eng:
            self.wait(e, [t for t in toks if t.sem is not self.sem[e]])
        self.keys = {k: v for k, v in self.keys.items() if k.startswith("xs")}


DBG_STOP = 0
import os as _os
EVAC_DVE = bool(int(_os.environ.get('EVAC_DVE', '0')))


def run_chains(factories, nblocks):
    free = list(range(nblocks))
    active = []
    pend = list(factories)
    while pend or active:
        while pend and free:
            blk = free.pop(0)
            active.append((pend.pop(0)(blk), blk))
        for item in list(active):
            g, blk = item
            try:
                next(g)
            except StopIteration:
                active.remove(item)
                free.append(blk)


def run_chains2(groups):
    st = [{"pend": list(f), "free": list(range(n)), "active": []} for f, n in groups]
    while any(g["pend"] or g["active"] for g in st):
        for g in st:
            while g["pend"] and g["free"]:
                blk = g["free"].pop(0)
                g["active"].append((g["pend"].pop(0)(blk), blk))
        for g in st:
            for item in list(g["active"]):
                gen, blk = item
                try:
                    next(gen)
                except StopIteration:
                    g["active"].remove(item)
                    g["free"].append(blk)


class TempPool:
    def __init__(self, sbf, name, n, width, dt):
        self.bufs = [sbf(f"{name}{i}", [128, width], dt) for i in range(n)]
        self.name = name
        self.i = 0

    def get(self):
        i = self.i
        self.i = (i + 1) % len(self.bufs)
        return self.bufs[i], f"{self.name}{i}"


def build_program(S, layers, do_final, TM=512, TF=256, only_mix=False):
    nc = bass.Bass("TRN2", target_bir_lowering=False)
    dt_in = lambda name, shape: nc.dram_tensor(name, shape, F32, kind="ExternalInput").ap()
    x_in = dt_in("x", [S, D])
    w_in_even = dt_in("w_in_even", [2, D, 2560])
    w_out_even = dt_in("w_out_even", [2, D, D])
    w_in_odd = dt_in("w_in_odd", [2, D, 2560])
    w_out_odd = dt_in("w_out_odd", [2, D, D])
    pool_w = dt_in("pool_w", [2, 4, 128, 128])
    w_up = dt_in("ffn_w_up", [DEPTH, D, FF])
    w_gate = dt_in("ffn_w_gate", [DEPTH, D, FF])
    w_down = dt_in("ffn_w_down", [DEPTH, FF, D])
    wbd_d = dt_in("wbd", [2, 2, 4, 128, 128])
    vecs_d = dt_in("vecs", [128, NV])
    gfin_d = dt_in("gfin", [128, D])
    consts_d = dt_in("consts", [128, NCONST])
    out_d = nc.dram_tensor("out", [S, D], F32, kind="ExternalOutput").ap()
    xs_d = nc.dram_tensor("xs", [S, D], F32, kind="Internal").ap()

    phases = []
    for l in layers:
        phases.append(("mix", l))
        if not only_mix:
            phases.append(("ffn", l))
    if do_final:
        phases.append(("final", -1))

    with ExitStack() as top:
        B = Builder(nc, top)
        sbt = lambda name, shape, dt: top.enter_context(nc.sbuf_tensor("sb_" + name, shape, dt))
        vecs = sbt("vecs", [128, NV], F32)
        dv = sbt("dv", [128, 64], F32)
        constf = sbt("constf", [128, NCONST], F32)
        ident = sbt("ident", [128, 128], BF16)
        ones_bf = sbt("ones_bf", [128, 128], BF16)
        ss = sbt("ss", [128, 4], F32)
        rs = sbt("rs", [128, 4], F32)
        trimask = constf[:, 128:256]
        cmask = constf[:, 256:768]
        poolinv = constf[:, 768:832]
        psb = [top.enter_context(nc.psum_tensor(f"psb{i}", [128, 1024], BF16)) for i in range(2)]
        psf = [top.enter_context(nc.psum_tensor(f"psf{i}", [128, 512], F32)) for i in range(6)]

        def V(name, c=0):
            o = VOFF[name] + c
            return vecs[:, o:o + 1]

        DVOFF = {}
        _n = 0
        for e in range(2):
            for nm in ("clam", "clam2"):
                DVOFF[f"{nm}{e}"] = _n
                _n += 4
        for o in range(2):
            for nm in ("lb", "oml", "noml"):
                DVOFF[f"{nm}{o}"] = _n
                _n += 4
        DVOFF["tmp"] = _n

        def DV(name, c=0):
            o = DVOFF[name] + c
            return dv[:, o:o + 1]

        B.dma("sp", "c0", vecs[:, :], vecs_d, w=["vecs"])
        B.dma("sp", "c1", constf[:, :], consts_d, w=["constf"])
        B.op("dve", lambda e: e.tensor_copy(ident[:, :], constf[:, 0:128]), r=["constf"], w=["ident"])
        B.op("dve", lambda e: e.tensor_copy(ones_bf[:, :], constf[:, 832:960]), r=["constf"], w=["ones"])
        for e_ in range(2):
            lo = DVOFF[f"clam{e_}"]
            lo2 = DVOFF[f"clam2{e_}"]
            lam = vecs[:, VOFF[f"lam{e_}"]:VOFF[f"lam{e_}"] + 4]
            B.op("act", lambda e, lo=lo, lam=lam: e.activation(dv[:, lo:lo + 4], lam, AF.Exp, scale=-1.0), r=["vecs"], w=["dv"])
            B.op("act", lambda e, lo=lo: e.activation(dv[:, lo:lo + 4], dv[:, lo:lo + 4], AF.Ln, bias=1.0), r=["dv"], w=["dv"])
            B.op("dve", lambda e, lo=lo, lo2=lo2: e.tensor_scalar(dv[:, lo2:lo2 + 4], dv[:, lo:lo + 4], -16.0, None, ALU.mult), r=["dv"], w=["dv"])
            B.op("dve", lambda e, lo=lo: e.tensor_scalar(dv[:, lo:lo + 4], dv[:, lo:lo + 4], -8.0, None, ALU.mult), r=["dv"], w=["dv"])
        l0 = DVOFF["lb0"]
        B.op("dve", lambda e: e.memset(dv[:, l0:l0 + 4], 0.0), w=["dv"])
        B.op("dve", lambda e: e.memset(dv[:, DVOFF["oml0"]:DVOFF["oml0"] + 4], 1.0), w=["dv"])
        B.op("dve", lambda e: e.memset(dv[:, DVOFF["noml0"]:DVOFF["noml0"] + 4], -1.0), w=["dv"])
        l1 = DVOFF["lb1"]
        tmpo = DVOFF["tmp"]
        B.op("dve", lambda e: e.tensor_tensor(dv[:, tmpo:tmpo + 4], vecs[:, VOFF["lbl1"]:VOFF["lbl1"] + 4],
                                              vecs[:, VOFF["lbl0"]:VOFF["lbl0"] + 4], ALU.subtract), r=["vecs", "dv"], w=["dv"])
        B.op("act", lambda e: e.activation(dv[:, l1:l1 + 4], dv[:, tmpo:tmpo + 4], AF.Sigmoid), r=["dv"], w=["dv"])
        B.op("dve", lambda e: e.tensor_scalar(dv[:, DVOFF["oml1"]:DVOFF["oml1"] + 4], dv[:, l1:l1 + 4], -1.0, 1.0, ALU.mult, ALU.add), r=["dv"], w=["dv"])
        B.op("dve", lambda e: e.tensor_scalar(dv[:, DVOFF["noml1"]:DVOFF["noml1"] + 4], dv[:, l1:l1 + 4], -1.0, None, ALU.add), r=["dv"], w=["dv"])
        B.barrier()
        B.stop_at(1)

        class Ctx:
            pass

        def rot(lst, name):
            st = {"i": 0}

            def nxt():
                i = st["i"]
                st["i"] = (i + 1) % len(lst)
                return lst[i], f"{name}{i}"
            return nxt

        def load_x(C, src, ti, T):
            NS = T // 128
            b = ti % 2
            tok0 = ti * T
            xk = [f"xs{j}" for j in range(tok0 // 256, (tok0 + T) // 256)]
            B.dma("sp", f"ldx{b}", C.xt[b][:, 0:NS, :], src[tok0:tok0 + T, :].rearrange("(s p) d -> p s d", p=128),
                  r=xk, w=[f"xt{b}"])

        def normA_gen(C, ti, T):
            NS = T // 128
            b = ti % 2
            xt = C.xt[b]
            xk = f"xt{b}"
            for s in range(NS):
                B.op("act", lambda e, s=s: e.activation(C.hb16[:, s, :], xt[:, s, :], AF.Square, accum_out=ss[:, s:s + 1]), r=[xk], w=["ss", f"hb{s}"])
                yield
            B.op("dve", lambda e: e.tensor_scalar(rs[:, 0:NS], ss[:, 0:NS], 1.0 / D, EPS, ALU.mult, ALU.add), r=["ss"], w=["rs"])
            B.op("act", lambda e: e.activation(rs[:, 0:NS], rs[:, 0:NS], AF.Sqrt), r=["rs"], w=["rs"])
            yield
            B.op("dve", lambda e: e.reciprocal(rs[:, 0:NS], rs[:, 0:NS]), r=["rs"], w=["rs"])
            yield
            for s in range(NS):
                if s % 2 == 0:
                    B.op("dve", lambda e, s=s: e.tensor_scalar(C.hb16[:, s, :], xt[:, s, :], rs[:, s:s + 1], None, ALU.mult),
                         r=[xk, "rs"], w=[f"hb{s}"])
                else:
                    B.op("act", lambda e, s=s: e.activation(C.hb16[:, s, :], xt[:, s, :], AF.Identity, scale=rs[:, s:s + 1]),
                         r=[xk, "rs"], w=[f"hb{s}"])
                yield

        def normB_gen(C, ti, T, gname):
            NS = T // 128
            b = ti % 2
            hT = C.hT[b]
            for kc in range(KC):
                reg, rk = C.ptr()
                for s in range(NS):
                    B.op("pe", lambda e, s=s, kc=kc, reg=reg: e.transpose(reg[:, s * 128:(s + 1) * 128], C.hb16[:, s, kc * 128:(kc + 1) * 128], ident[:, :]),
                         r=[f"hb{s}", "ident"] if s == 0 else [f"hb{s}"], w=[rk] if s == 0 else (), inc=(s == NS - 1))
                g = V(gname, kc)
                if kc % 2 == 0 or EVAC_DVE:
                    B.op("dve", lambda e, kc=kc, reg=reg, g=g: e.tensor_scalar(hT[:, kc, 0:T], reg[:, 0:T], g, None, ALU.mult),
                         r=[rk, "vecs"], w=[f"hT{b}_{kc}"])
                else:
                    B.op("act", lambda e, kc=kc, reg=reg, g=g: e.activation(hT[:, kc, 0:T], reg[:, 0:T], AF.Identity, scale=g),
                         r=[rk, "vecs"], w=[f"hT{b}_{kc}"])
                yield

        def run_gen(g):
            for _ in g:
                pass

        def interleave(gens):
            gens = list(gens)
            while gens:
                for g in list(gens):
                    try:
                        next(g)
                    except StopIteration:
                        gens.remove(g)

        def tile_begin(C, src, ti, T, NT, gname):
            if ti == 0:
                C.oproj = {}
                load_x(C, src, 0, T)
                run_gen(normA_gen(C, 0, T))
                run_gen(normB_gen(C, 0, T, gname))
                if NT > 1:
                    load_x(C, src, 1, T)
            if ti + 2 < NT:
                load_x(C, src, ti + 2, T)
            C.next_normA = (lambda blk=None: normA_gen(C, ti + 1, T)) if ti + 1 < NT else None

        def tile_end(C, ti, T, NT, gname, oproj_factory, depth):
            C.oproj[ti] = oproj_factory
            gens = []
            if ti + 1 < NT:
                gens.append(normB_gen(C, ti + 1, T, gname))
            if ti - depth >= 0:
                gens.append(C.oproj.pop(ti - depth)())
            interleave(gens)

        def tile_finish(C, NT):
            for j in sorted(C.oproj):
                run_gen(C.oproj[j]())
            C.oproj = {}

        def hT_keys(b):
            return [f"hT{b}_{kc}" for kc in range(KC)]

        pending = []

        def defer(fn):
            if pending:
                pending.pop()()
            pending.append(fn)

        def flush():
            while pending:
                pending.pop()()

        xr_state = {"i": 0}

        def out_proj_gen(C, src, dst, ti, T, yT, ykeys, nk, wt, wkeys_for_half, chunk):
            NS = T // 128
            tok0 = ti * T
            for s in range(NS):
                j = xr_state["i"]
                xr_state["i"] = (j + 1) % len(C.xr)
                xr = C.xr[j]
                xrk = f"xr{j}"
                t0 = tok0 + s * 128
                xk = [f"xs{t0 // 256}"]
                B.dma("sp", f"ldr{j}", xr[:, :], src[t0:t0 + 128, :], r=xk, w=[xrk])
                for half in range(2):
                    pb, pk = C.gp()
                    yield from B.mm_gen(pb[:, :], pk, [(yT[:, kc, s * 128:(s + 1) * 128], wt[:, kc, half * 512:(half + 1) * 512]) for kc in range(nk)],
                                        list(wkeys_for_half(half)), chunk, pair_r=list(ykeys))
                    B.op("dve", lambda e, half=half, pb=pb, xr=xr: e.tensor_tensor(xr[:, half * 512:(half + 1) * 512], pb[:, :],
                                                                                  xr[:, half * 512:(half + 1) * 512], ALU.add),
                         r=[pk, xrk], w=[xrk])
                    yield
                B.dma("pool", f"str{j}", dst[t0:t0 + 128, :], xr[:, :], r=[xrk], w=xk)

        def load_w(dst_sb, src2d, rows_per_part, colblocks, keybase):
            nkc = rows_per_part
            for j, (c0, c1) in enumerate(colblocks):
                B.dma("pool", f"{keybase}{j}", dst_sb[:, 0:nkc, c0:c1], src2d[:, c0:c1].rearrange("(kc p) n -> p kc n", p=128),
                      w=[f"{keybase}{j}"])

        src = x_in
        for pi, (kind, l) in enumerate(phases):
            last = pi == len(phases) - 1
            dst = out_d if last else xs_d
            with ExitStack() as ph:
                sbf = lambda name, shape, dt, ph=ph, pi=pi: ph.enter_context(nc.sbuf_tensor(f"p{pi}_{name}", shape, dt))
                C = Ctx()
                C.ptr = rot([psb[0][:, 0:512], psb[1][:, 0:512]], "ptr")
                if kind == "final":
                    T = TM
                    NS = T // 128
                    C.xt = [sbf(f"xt{i}", [128, NS, D], F32) for i in range(2)]
                    C.junk = sbf("junk", [128, D], BF16)
                    gfin = sbf("gfin", [128, D], F32)
                    B.dma("sp", "c2", gfin[:, :], gfin_d, w=["gfin"])
                    NT = S // T
                    load_x(C, src, 0, T)
                    for ti in range(NT):
                        b = ti % 2
                        if ti + 1 < NT:
                            load_x(C, src, ti + 1, T)
                        xt = C.xt[b]
                        xk = f"xt{b}"
                        for s in range(NS):
                            B.op("act", lambda e, s=s: e.activation(C.junk[:, :], xt[:, s, :], AF.Square, accum_out=ss[:, s:s + 1]), r=[xk], w=["ss"])
                        B.op("dve", lambda e: e.tensor_scalar(rs[:, 0:NS], ss[:, 0:NS], 1.0 / D, EPS, ALU.mult, ALU.add), r=["ss"], w=["rs"])
                        B.op("act", lambda e: e.activation(rs[:, 0:NS], rs[:, 0:NS], AF.Sqrt), r=["rs"], w=["rs"])
                        B.op("dve", lambda e: e.reciprocal(rs[:, 0:NS], rs[:, 0:NS]), r=["rs"], w=["rs"])
                        for s in range(NS):
                            B.op("dve", lambda e, s=s: e.scalar_tensor_tensor(xt[:, s, :], xt[:, s, :], rs[:, s:s + 1], gfin[:, :], ALU.mult, ALU.mult),
                                 r=[xk, "rs", "gfin"], w=[xk])
                        tok0 = ti * T
                        xkeys = [f"xs{j}" for j in range(tok0 // 256, (tok0 + T) // 256)]
                        B.dma("pool", f"stx{b}", dst[tok0:tok0 + T, :].rearrange("(s p) d -> p s d", p=128), xt[:, 0:NS, :], r=[xk], w=xkeys)

                elif kind == "ffn":
                    T = TF
                    NS = T // 128
                    NT = S // T
                    wu = sbf("wu", [128, KC, FF], BF16)
                    wg = sbf("wg", [128, KC, FF], BF16)
                    wd = sbf("wd", [128, FC, D], BF16)
                    half = FF // 2
                    for j, (c0, c1) in enumerate([(0, 512), (512, half), (half, FF)]):
                        for nm, dsb, srcw in (("wu", wu, w_up[l]), ("wg", wg, w_gate[l])):
                            B.dma("pool", f"{nm}{j}", dsb[:, 0:KC, c0:c1], srcw[:, c0:c1].rearrange("(kc p) n -> p kc n", p=128), w=[f"{nm}{j}"])
                    for j, (k0, k1) in enumerate([(0, 11), (11, 22)]):
                        B.dma("pool", f"wd{j}", wd[:, k0:k1, :], w_down[l][k0 * 128:k1 * 128, :].rearrange("(kc p) n -> p kc n", p=128), w=[f"wd{j}"])

                    def ukey(base, oc):
                        c = oc * 128
                        return f"{base}{0 if c < 512 else (1 if c < half else 2)}"
                    C.xt = [sbf(f"xt{i}", [128, NS, D], F32) for i in range(2)]
                    C.xr = [sbf(f"xr{i}", [128, D], F32) for i in range(2)]
                    C.hb16 = sbf("hb16", [128, NS, D], BF16)
                    C.hT = [sbf(f"hT{i}", [128, KC, T], BF16) for i in range(2)]
                    vTs = [sbf(f"vT{i}", [128, FC, T], BF16) for i in range(2)]
                    uh = sbf("uh", [128, FC, 2], F32)
                    tp = TempPool(sbf, "tf", 6, T + 2, F32)
                    C.gp = rot([p for p in psf], "gp")
                    B.op("pool", lambda e: e.memset(uh[:, :, :], 0.0), w=["uh"])
                    for ti in range(NT):
                        b = ti % 2
                        tile_begin(C, src, ti, T, NT, f"gffn{l}")
                        hT = C.hT[b]
                        vT = vTs[b]
                        def ffn_chain(oc, blk, b=b, hT=hT, vT=vT):
                            pu, puk = C.gp()
                            B.mm(pu[:, 0:T], puk, [(wu[:, kc, oc * 128:(oc + 1) * 128], hT[:, kc, 0:T]) for kc in range(KC)],
                                 r=hT_keys(b) + [ukey("wu", oc)])
                            yield
                            pg, pgk = C.gp()
                            B.mm(pg[:, 0:T], pgk, [(wg[:, kc, oc * 128:(oc + 1) * 128], hT[:, kc, 0:T]) for kc in range(KC)],
                                 r=hT_keys(b) + [ukey("wg", oc)])
                            yield
                            ue, uek = tp.bufs[2 * blk], f"tf{2 * blk}"
                            cv, cvk = tp.bufs[2 * blk + 1], f"tf{2 * blk + 1}"
                            B.op("pool", lambda e: e.tensor_copy(ue[:, 0:2], uh[:, oc, :]), r=["uh"], w=[uek + "h"])
                            B.op("act", lambda e: e.activation(ue[:, 2:2 + T], pu[:, 0:T], AF.Copy), r=[puk], w=[uek])
                            yield
                            B.op("pool", lambda e: e.tensor_copy(uh[:, oc, :], ue[:, T:T + 2]), r=[uek], w=["uh"])
                            B.op("dve", lambda e: e.tensor_scalar(cv[:, 0:T], ue[:, 2:2 + T], V(f"fcw{l}_2", oc), V(f"fcb{l}", oc), ALU.mult, ALU.add),
                                 r=[uek, "vecs"], w=[cvk])
                            yield
                            B.op("dve", lambda e: e.scalar_tensor_tensor(cv[:, 0:T], ue[:, 1:1 + T], V(f"fcw{l}_1", oc), cv[:, 0:T], ALU.mult, ALU.add),
                                 r=[uek, uek + "h", cvk], w=[cvk])
                            yield
                            B.op("dve", lambda e: e.scalar_tensor_tensor(cv[:, 0:T], ue[:, 0:T], V(f"fcw{l}_0", oc), cv[:, 0:T], ALU.mult, ALU.add),
                                 r=[uek, uek + "h", cvk], w=[cvk])
                            yield
                            B.op("act", lambda e: e.activation(cv[:, 0:T], cv[:, 0:T], AF.Gelu_apprx_tanh), r=[cvk], w=[cvk])
                            yield
                            B.op("dve", lambda e: e.tensor_tensor(vT[:, oc, :], pg[:, 0:T], cv[:, 0:T], ALU.mult),
                                 r=[pgk, cvk], w=[f"vT{b}_{oc}"])
                            yield

                        facs = [(lambda blk, oc=oc: ffn_chain(oc, blk)) for oc in range(FC)]
                        grp = [(facs, 3)]
                        if C.next_normA is not None:
                            grp.append(([C.next_normA], 1))
                        run_chains2(grp)
                        tile_end(C, ti, T, NT, f"gffn{l}",
                                 (lambda ti=ti, vT=vT, b=b: out_proj_gen(C, src, dst, ti, T, vT, [f"vT{b}_{oc}" for oc in range(FC)], FC, wd,
                                                                         lambda hf: ["wd0", "wd1"], 11)), 0)
                    tile_finish(C, NT)

                else:
                    T = TM
                    NS = T // 128
                    NT = S // T
                    even = (l % 2 == 0)
                    e_ = l // 2
                    wi = sbf("wi", [128, KC, 2560], BF16)
                    wo = sbf("wo", [128, KC, D], BF16)
                    load_w(wi, (w_in_even if even else w_in_odd)[e_], KC, [(j * 512, (j + 1) * 512) for j in range(5)], "wi")
                    load_w(wo, (w_out_even if even else w_out_odd)[e_], KC, [(0, 512), (512, 1024)], "wo")
                    B.stop_at(2)
                    C.xt = [sbf(f"xt{i}", [128, NS, D], F32) for i in range(2)]
                    C.xr = [sbf(f"xr{i}", [128, D], F32) for i in range(2)]
                    C.hb16 = sbf("hb16", [128, NS, D], BF16)
                    C.hT = [sbf(f"hT{i}", [128, KC, T], BF16) for i in range(2)]
                    yTs = [sbf(f"yT{i}", [128, KC, T], BF16) for i in range(2)]

                    def proj(oc, hb_, gp):
                        pb, pk = gp()
                        B.mm(pb[:, 0:T], pk, [(wi[:, kc, oc * 128:(oc + 1) * 128], C.hT[hb_][:, kc, 0:T]) for kc in range(KC)],
                             r=hT_keys(hb_) + [f"wi{oc // 4}"])
                        return pb, pk

                    if even:
                        wbd = sbf("wbd", [128, 2, 4, 128], BF16)
                        for gi in range(2):
                            B.dma("pool", f"wbd{gi}", wbd[:, gi, :, :], wbd_d[e_, gi].rearrange("c p j -> p c j"), w=[f"wbd{gi}"])
                        xa = [sbf(f"xa{c}", [128, T + 3], F32) for c in range(4)]
                        pe_ = [sbf(f"pe{c}", [128, T + 2], F32) for c in range(4)]
                        gg = [sbf(f"gg{c}", [128, T], F32) for c in range(4)]
                        hst = sbf("hst", [128, 4], F32)
                        tp = TempPool(sbf, "tm", 6, T, F32)
                        tp2 = TempPool(sbf, "tn", 2, T, F32)
                        tps = TempPool(sbf, "ts", 2, T, F32)
                        tpb = TempPool(sbf, "tb", 2, T, BF16)
                        C.gp = rot([p for p in psf], "gp")
                        for c in range(4):
                            B.op("pool", lambda e, c=c: e.memset(xa[c][:, 0:3], 0.0), w=[f"xa{c}h"])
                            B.op("pool", lambda e, c=c: e.memset(pe_[c][:, 0:2], 0.0), w=[f"pe{c}h"])
                        for ti in range(NT):
                            b = ti % 2
                            tile_begin(C, src, ti, T, NT, f"gmix{l}")
                            yT = yTs[b]
                            def lru_chain(c, blk, b=b, yT=yT, ti=ti):
                                xc, xck = tp.bufs[3 * blk], f"tm{3 * blk}"
                                rt, rtk = tp.bufs[3 * blk + 1], f"tm{3 * blk + 1}"
                                at, atk = tp.bufs[3 * blk + 2], f"tm{3 * blk + 2}"
                                ig, igk = tp2.bufs[blk], f"tn{blk}"
                                xcb, xcbk = tpb.bufs[blk], f"tb{blk}"
                                xak = f"xa{c}"
                                pb, pk = proj(c, b, C.gp)
                                B.op("act", lambda e: e.activation(xa[c][:, 3:3 + T], pb[:, 0:T], AF.Copy), r=[pk], w=[xak])
                                yield
                                pb2, pk2 = proj(4 + c, b, C.gp)
                                B.op("act", lambda e: e.activation(gg[c][:, :], pb2[:, 0:T], AF.Gelu_apprx_tanh), r=[pk2], w=[f"gg{c}"])
                                yield
                                B.op("dve", lambda e: e.tensor_scalar(xc[:, :], xa[c][:, 3:3 + T], V(f"lcw{e_}_3", c), V(f"lcb{e_}", c), ALU.mult, ALU.add),
                                     r=[xak, "vecs"], w=[xck])
                                yield
                                for j in (2, 1, 0):
                                    B.op("dve", lambda e, j=j: e.scalar_tensor_tensor(xc[:, :], xa[c][:, j:j + T], V(f"lcw{e_}_{j}", c), xc[:, :], ALU.mult, ALU.add),
                                         r=[xak, xak + "h", xck], w=[xck])
                                    yield
                                B.op("pool", lambda e: e.tensor_copy(xa[c][:, 0:3], xa[c][:, T:T + 3]), r=[xak], w=[xak + "h"])
                                B.op("act", lambda e: e.activation(xcb[:, :], xc[:, :], AF.Copy), r=[xck], w=[xcbk])
                                yield
                                pr, prk = C.gp()
                                B.mm(pr[:, 0:T], prk, [(wbd[:, 0, c, :], xcb[:, :])], r=[xcbk, "wbd0"])
                                pi_, pik = C.gp()
                                B.mm(pi_[:, 0:T], pik, [(wbd[:, 1, c, :], xcb[:, :])], r=[xcbk, "wbd1"])
                                yield
                                B.op("act", lambda e: e.activation(rt[:, :], pr[:, 0:T], AF.Sigmoid, bias=V(f"lba{e_}", c)), r=[prk, "vecs"], w=[rtk])
                                yield
                                B.op("act", lambda e: e.activation(ig[:, :], pi_[:, 0:T], AF.Sigmoid, bias=V(f"lbi{e_}", c)), r=[pik, "vecs"], w=[igk])
                                yield
                                B.op("dve", lambda e: e.tensor_tensor(xc[:, :], ig[:, :], xc[:, :], ALU.mult), r=[igk, xck], w=[xck])
                                yield
                                B.op("act", lambda e: e.activation(at[:, :], rt[:, :], AF.Exp, scale=DV(f"clam{e_}", c)), r=[rtk, "dv"], w=[atk])
                                yield
                                B.op("act", lambda e: e.activation(rt[:, :], rt[:, :], AF.Exp, scale=DV(f"clam2{e_}", c)), r=[rtk, "dv"], w=[rtk])
                                yield
                                B.op("act", lambda e: e.activation(rt[:, :], rt[:, :], AF.Sqrt, bias=1.0, scale=-1.0), r=[rtk], w=[rtk])
                                yield
                                if ti == 0:
                                    B.op("dve", lambda e: e.memset(rt[:, 0:1], 1.0), r=[rtk], w=[rtk])
                                B.op("dve", lambda e: e.tensor_tensor(xc[:, :], rt[:, :], xc[:, :], ALU.mult), r=[rtk, xck], w=[xck])
                                yield
                                init = 0.0 if ti == 0 else hst[:, c:c + 1]
                                B.op("dve", lambda e: e.tensor_tensor_scan(rt[:, :], at[:, :], xc[:, :], init, ALU.mult, ALU.add),
                                     r=[atk, xck, f"hst{c}"], w=[rtk])
                                yield
                                B.op("dve", lambda e: e.tensor_copy(hst[:, c:c + 1], rt[:, T - 1:T]), r=[rtk], w=[f"hst{c}"])
                                B.op("dve", lambda e: e.tensor_tensor(yT[:, c, :], rt[:, :], gg[c][:, :], ALU.mult), r=[rtk, f"gg{c}"], w=[f"yT{b}_{c}"])
                                yield

                            def sconv_chain(c, blk, b=b, yT=yT):
                                hbx, hbk = tps.bufs[2 * blk], f"ts{2 * blk}"
                                cv, cvk = tps.bufs[2 * blk + 1], f"ts{2 * blk + 1}"
                                pek = f"pe{c}"
                                pb, pk = proj(8 + c, b, C.gp)
                                B.op("act", lambda e: e.activation(hbx[:, :], pb[:, 0:T], AF.Copy), r=[pk], w=[hbk])
                                yield
                                pb2, pk2 = proj(16 + c, b, C.gp)
                                B.op("dve", lambda e: e.tensor_tensor(pe_[c][:, 2:2 + T], pb2[:, 0:T], hbx[:, :], ALU.mult), r=[pk2, hbk], w=[pek])
                                yield
                                B.op("dve", lambda e: e.tensor_scalar(cv[:, :], pe_[c][:, 2:2 + T], V(f"scw{e_}_2", c), None, ALU.mult), r=[pek, "vecs"], w=[cvk])
                                yield
                                B.op("dve", lambda e: e.scalar_tensor_tensor(cv[:, :], pe_[c][:, 1:1 + T], V(f"scw{e_}_1", c), cv[:, :], ALU.mult, ALU.add), r=[pek, pek + "h", cvk], w=[cvk])
                                yield
                                B.op("dve", lambda e: e.scalar_tensor_tensor(cv[:, :], pe_[c][:, 0:T], V(f"scw{e_}_0", c), cv[:, :], ALU.mult, ALU.add), r=[pek, pek + "h", cvk], w=[cvk])
                                yield
                                B.op("pool", lambda e: e.tensor_copy(pe_[c][:, 0:2], pe_[c][:, T:T + 2]), r=[pek], w=[pek + "h"])
                                pb3, pk3 = proj(12 + c, b, C.gp)
                                B.op("dve", lambda e: e.tensor_tensor(yT[:, 4 + c, :], pb3[:, 0:T], cv[:, :], ALU.mult), r=[pk3, cvk], w=[f"yT{b}_{4 + c}"])
                                yield

                            lru_f = [(lambda blk, c=c: lru_chain(c, blk)) for c in range(4)]
                            sc_f = [(lambda blk, c=c: sconv_chain(c, blk)) for c in range(4)]
                            grp = [(lru_f, 2), (sc_f, 1)]
                            if C.next_normA is not None:
                                grp.append(([C.next_normA], 1))
                            run_chains2(grp)
                            tile_end(C, ti, T, NT, f"gmix{l}",
                                     (lambda ti=ti, yT=yT, b=b: out_proj_gen(C, src, dst, ti, T, yT, [f"yT{b}_{c}" for c in range(KC)], KC, wo, lambda hf: [f"wo{hf}"], 8)), 1)
                        tile_finish(C, NT)

                    else:
                        o_ = e_
                        pw = sbf("pw", [128, 4, 128], BF16)
                        B.dma("pool", "pw", pw[:, :, :], pool_w[o_].rearrange("g p j -> p g j"), w=["pw"])
                        ue = [sbf(f"ue{c}", [128, T + 16], F32) for c in range(4)]
                        Vt = sbf("Vt", [128, NS, 512], BF16)
                        Sst = sbf("Sst", [128, 4, 128], F32)
                        Sb = [sbf(f"Sb{i}", [128, 4, 128], BF16) for i in range(2)]
                        tp = TempPool(sbf, "tm", 16, T + 16, F32)
                        tpb = TempPool(sbf, "tb", 10, T, BF16)
                        tam = TempPool(sbf, "am", 4, 128, BF16)
                        gpl = rot([psf[0], psf[1]], "gp")
                        C.gp = gpl
                        pa_r = rot([psf[2][:, i * 128:(i + 1) * 128] for i in range(4)], "pa")
                        pd_r = rot([psf[3][:, i * 128:(i + 1) * 128] for i in range(4)], "pd")
                        po = [psf[4], psf[5]]
                        for c in range(4):
                            B.op("pool", lambda e, c=c: e.memset(ue[c][:, 0:16], 0.0), w=[f"ue{c}h"])
                        B.op("pool", lambda e: e.memset(Sst[:, :, :], 0.0), w=[f"S{h}" for h in range(4)])
                        B.op("pool", lambda e: e.memset(Sb[0][:, :, :], 0.0), w=[f"Sb0_{h}" for h in range(4)])
                        sbpar = [0, 0, 0, 0]
                        for ti in range(NT):
                            b = ti % 2
                            tile_begin(C, src, ti, T, NT, f"gmix{l}")
                            yT = yTs[b]
                            def pool_chain(c, blk, b=b, yT=yT, ti=ti):
                                win = POOL_WINS[c]
                                uk = f"ue{c}"
                                W = T + 16
                                tb_ = [(tp.bufs[12 + 2 * blk + k], f"tm{12 + 2 * blk + k}") for k in range(2)]
                                pp, ppk = tpb.bufs[8 + blk], f"tb{8 + blk}"
                                pb, pk = proj(c, b, gpl)
                                B.op("act", lambda e: e.activation(ue[c][:, 16:16 + T], pb[:, 0:T], AF.Copy), r=[pk], w=[uk])
                                yield
                                prev, prevk = ue[c], uk
                                first = True
                                step = 1
                                k = 0
                                while step < win:
                                    nx, nxk = tb_[k % 2]
                                    k += 1
                                    B.op("dve", lambda e, nx=nx, prev=prev, step=step: e.tensor_tensor(nx[:, 2 * step - 1:W], prev[:, 2 * step - 1:W], prev[:, step - 1:W - step], ALU.add),
                                         r=[prevk, uk + "h"] if first else [prevk], w=[nxk])
                                    yield
                                    first = False
                                    prev, prevk = nx, nxk
                                    step *= 2
                                if ti == 0:
                                    t16, t16k = tb_[k % 2]
                                    B.op("dve", lambda e: e.tensor_tensor(t16[:, 0:16], prev[:, 16:32], poolinv[:, c * 16:(c + 1) * 16], ALU.mult),
                                         r=[prevk, "constf"], w=[t16k])
                                    B.op("dve", lambda e: e.tensor_tensor(pp[:, 0:16], t16[:, 0:16], ue[c][:, 16:32], ALU.subtract),
                                         r=[t16k, uk], w=[ppk])
                                    B.op("dve", lambda e: e.scalar_tensor_tensor(pp[:, 16:T], prev[:, 32:16 + T], 1.0 / win, ue[c][:, 32:16 + T], ALU.mult, ALU.subtract),
                                         r=[prevk, uk, ppk], w=[ppk])
                                else:
                                    B.op("dve", lambda e: e.scalar_tensor_tensor(pp[:, 0:T], prev[:, 16:16 + T], 1.0 / win, ue[c][:, 16:16 + T], ALU.mult, ALU.subtract),
                                         r=[prevk, uk], w=[ppk])
                                yield
                                B.op("pool", lambda e: e.tensor_copy(ue[c][:, 0:16], ue[c][:, T:T + 16]), r=[uk], w=[uk + "h"])
                                py, pyk = gpl()
                                B.mm(py[:, 0:T], pyk, [(pw[:, c, :], pp[:, 0:T])], r=[ppk, "pw"])
                                B.op("act", lambda e: e.activation(yT[:, c, :], py[:, 0:T], AF.Identity, scale=V(f"psc{o_}", c)), r=[pyk, "vecs"], w=[f"yT{b}_{c}"])
                                yield

                            def head_prep(hd, blk, h, b=b):
                                fb = lambda k: (tp.bufs[6 * blk + k], f"tm{6 * blk + k}")
                                bb = lambda k: (tpb.bufs[4 * blk + k], f"tb{4 * blk + k}")
                                h.q, h.qk = fb(0)
                                h.sg, h.sgk = fb(1)
                                h.lf, h.lfk = fb(2)
                                h.G, h.Gk = fb(3)
                                h.Eg, h.Egk = fb(4)
                                h.sd, h.sdk = fb(5)
                                h.Qt, h.Qtk = bb(0)
                                h.Kt, h.Ktk = bb(1)
                                h.Kh, h.Khk = bb(2)
                                h.KhT, h.KhTk = bb(3)
                                pb, pk = proj(4 + hd, b, gpl)
                                B.op("act", lambda e: e.activation(h.q[:, 0:T], pb[:, 0:T], AF.Copy), r=[pk], w=[h.qk])
                                yield
                                pb2, pk2 = proj(8 + hd, b, gpl)
                                B.op("act", lambda e: e.activation(h.sg[:, 0:T], pb2[:, 0:T], AF.Sigmoid), r=[pk2], w=[h.sgk])
                                yield
                                B.op("dve", lambda e: e.tensor_scalar(h.lf[:, 0:T], h.sg[:, 0:T], DV(f"oml{o_}", hd), DV(f"lb{o_}", hd), ALU.mult, ALU.add),
                                     r=[h.sgk, "dv"], w=[h.lfk])
                                yield
                                B.op("act", lambda e: e.activation(h.lf[:, 0:T], h.lf[:, 0:T], AF.Ln), r=[h.lfk], w=[h.lfk])
                                B.op("dve", lambda e: e.tensor_scalar(h.sg[:, 0:T], h.sg[:, 0:T], DV(f"noml{o_}", hd), DV(f"oml{o_}", hd), ALU.mult, ALU.add),
                                     r=[h.sgk, "dv"], w=[h.sgk])
                                yield
                                B.op("dve", lambda e: e.tensor_tensor_scan(h.G[:, 0:T], cmask[:, 0:T], h.lf[:, 0:T], 0.0, ALU.mult, ALU.add),
                                     r=[h.lfk, "constf"], w=[h.Gk])
                                yield
                                B.op("act", lambda e: e.activation(h.Eg[:, 0:T], h.G[:, 0:T], AF.Exp), r=[h.Gk], w=[h.Egk])
                                yield
                                B.op("act", lambda e: e.activation(h.G[:, 0:T], h.G[:, 0:T], AF.Exp, scale=-1.0), r=[h.Gk], w=[h.Gk])
                                B.op("dve", lambda e: e.tensor_tensor(h.Qt[:, 0:T], h.q[:, 0:T], h.Eg[:, 0:T], ALU.mult), r=[h.qk, h.Egk], w=[h.Qtk])
                                yield
                                B.op("dve", lambda e: e.tensor_tensor(h.sg[:, 0:T], h.sg[:, 0:T], h.G[:, 0:T], ALU.mult), r=[h.sgk, h.Gk], w=[h.sgk])
                                yield
                                B.op("act", lambda e: e.activation(h.Kt[:, 0:T], h.sg[:, 0:T], AF.Copy), r=[h.sgk], w=[h.Ktk])
                                B.op("dve", lambda e: e.tensor_tensor(
                                    h.Kh[:, 0:T].rearrange("p (c l) -> p c l", l=64), h.sg[:, 0:T].rearrange("p (c l) -> p c l", l=64),
                                    h.Eg[:, 0:T].rearrange("p (c l) -> p c l", l=64)[:, :, 63:64].to_broadcast([128, T // 64, 64]), ALU.mult),
                                    r=[h.sgk, h.Egk], w=[h.Khk])
                                yield
                                pb3, pk3 = proj(16 + hd, b, gpl)
                                B.op("act", lambda e: e.activation(h.sd[:, 0:T], pb3[:, 0:T], AF.Silu), r=[pk3], w=[h.sdk])
                                yield
                                reg, rk = C.ptr()
                                for s in range(NS):
                                    B.op("pe", lambda e, s=s: e.transpose(reg[:, s * 128:(s + 1) * 128], h.Kh[:, s * 128:(s + 1) * 128], ident[:, :]),
                                         r=[h.Khk, "ident"] if s == 0 else (), w=[rk] if s == 0 else (), inc=(s == NS - 1))
                                B.op("act", lambda e: e.activation(h.KhT[:, 0:T], reg[:, 0:T], AF.Copy), r=[rk], w=[h.KhTk])
                                yield

                            for s in range(NS):
                                pv, pvk = gpl()
                                B.mm(pv[:, :], pvk, [(C.hT[b][:, kc, s * 128:(s + 1) * 128], wi[:, kc, 1536:2048]) for kc in range(KC)],
                                     r=hT_keys(b) + ["wi3"])
                                B.op("act", lambda e, s=s, pv=pv: e.activation(Vt[:, s, :], pv[:, :], AF.Copy), r=[pvk], w=[f"Vt{s}"])
                            def head_chain(hd, blk, b=b, yT=yT):
                                h = Ctx()
                                yield from head_prep(hd, blk, h)
                                pob = po[blk]
                                pok = f"po{blk}"
                                hbank = psf[2 + blk]
                                hbk = f"pab{blk}"
                                for dc in range(NS):
                                    sl = slice(dc * 128, (dc + 1) * 128)
                                    pa, pak = hbank[:, 0:128], hbk
                                    B.mm(pa, pak, [(h.Kt[:, sl], h.Qt[:, sl])], r=[h.Ktk, h.Qtk])
                                    yield
                                    am, amk = tam.get()
                                    B.op("dve", lambda e: e.tensor_tensor(am[:, :], pa, trimask, ALU.mult), r=[pak, "constf"], w=[amk])
                                    yield
                                    B.op("pe", lambda e: e.matmul(pob[:, sl], Vt[:, dc, hd * 128:(hd + 1) * 128], am[:, :], start=True, stop=False),
                                         r=[f"Vt{dc}", amk], w=[pok], inc=False)
                                    for c2 in range(2):
                                        ch = dc * 2 + c2
                                        par = sbpar[hd]
                                        csl = slice(ch * 64, (ch + 1) * 64)
                                        B.op("pe", lambda e: e.matmul(pob[:, csl], Sb[par][:, hd, :], h.Qt[:, csl], start=False, stop=(c2 == 1)),
                                             r=[f"Sb{par}_{hd}", h.Qtk], w=[pok], inc=True)
                                        if c2 == 0:
                                            yield
                                        pd, pdk = hbank[:, 128:256], hbk
                                        rows = slice(c2 * 64, (c2 + 1) * 64)
                                        B.mm(pd, pdk, [(h.KhT[rows, dc * 128:(dc + 1) * 128], Vt[rows, dc, hd * 128:(hd + 1) * 128])], r=[h.KhTk, f"Vt{dc}"])
                                        yield
                                        dcol = ch * 64 + 63
                                        B.op("dve", lambda e: e.scalar_tensor_tensor(Sst[:, hd, :], Sst[:, hd, :], h.Eg[:, dcol:dcol + 1], pd, ALU.mult, ALU.add),
                                             r=[pdk, h.Egk, f"S{hd}"], w=[f"S{hd}"])
                                        yield
                                        npar = 1 - par
                                        B.op("act", lambda e: e.activation(Sb[npar][:, hd, :], Sst[:, hd, :], AF.Copy), r=[f"S{hd}"], w=[f"Sb{npar}_{hd}"])
                                        sbpar[hd] = npar
                                        yield
                                osq, osqk = h.q, h.qk
                                B.op("act", lambda e: e.activation(osq[:, 0:T], pob[:, 0:T], AF.Square), r=[pok], w=[osqk])
                                yield
                                pst, pstk = gpl()
                                B.mm(pst[:, 0:T], pstk, [(constf[:, 832:960], osq[:, 0:T])], r=[osqk, "constf"])
                                rst, rstk = h.lf, h.lfk
                                B.op("act", lambda e: e.activation(rst[:, 0:T], pst[:, 0:T], AF.Sqrt, bias=EPS), r=[pstk], w=[rstk])
                                yield
                                B.op("dve", lambda e: e.reciprocal(rst[:, 0:T], rst[:, 0:T]), r=[rstk], w=[rstk])
                                yield
                                B.op("dve", lambda e: e.scalar_tensor_tensor(rst[:, 0:T], pob[:, 0:T], V(f"hng{o_}", hd), rst[:, 0:T], ALU.mult, ALU.mult),
                                     r=[pok, rstk, "vecs"], w=[rstk])
                                yield
                                B.op("dve", lambda e: e.tensor_tensor(yT[:, 4 + hd, :], rst[:, 0:T], h.sd[:, 0:T], ALU.mult), r=[rstk, h.sdk], w=[f"yT{b}_{4 + hd}"])
                                yield

                            grp = [([(lambda blk, hd=hd: head_chain(hd, blk)) for hd in range(4)], int(_os.environ.get("ODD_HB", "2"))),
                                   ([(lambda blk, c=c: pool_chain(c, blk)) for c in range(4)], int(_os.environ.get("ODD_PB", "2")))]
                            if C.next_normA is not None:
                                grp.append(([C.next_normA], 1))
                            run_chains2(grp)
                            tile_end(C, ti, T, NT, f"gmix{l}",
                                     (lambda ti=ti, yT=yT, b=b: out_proj_gen(C, src, dst, ti, T, yT, [f"yT{b}_{c}" for c in range(KC)], KC, wo, lambda hf: [f"wo{hf}"], 8)), 1)
                        tile_finish(C, NT)
                B.barrier()
            src = dst
        B.barrier(force=True)
    return nc, B.ninstr


_PROG_CACHE = {}


def _get_prog(S, layers, do_final):
    key = (S, tuple(layers), do_final)
    if key not in _PROG_CACHE:
        _PROG_CACHE[key] = build_program(S, list(layers), do_final)[0]
    return _PROG_CACHE[key]


def run_layers(xb, packed, layers, do_final, S):
    nc = _get_prog(S, layers, do_final)
    n = xb.shape[0]
    real = [0, 1, 4, 5][:n]
    zero = {k: np.zeros_like(v) for k, v in packed.items()}
    zero["x"] = np.zeros_like(xb[0])
    in_maps = []
    for i in range(8):
        if i in real:
            m = dict(packed)
            m["x"] = np.ascontiguousarray(xb[real.index(i)])
        else:
            m = zero
        in_maps.append(m)
    res = run_bass_kernel_spmd(nc, in_maps, core_ids=list(range(8)))
    return np.stack([res.results[i]["out"] for i in real])


def kernel(**inputs):
    packed = host_pack(inputs)
    x = np.asarray(inputs["x"], np.float32)
    out = run_layers(x, packed, [0, 1, 2, 3], True, SEQ)
    return out.astype(np.float32)
```

```python
from contextlib import ExitStack
import numpy as np
import concourse.bass as bass
import concourse.mybir as mybir
from concourse.bass_utils import run_bass_kernel_spmd

F32 = mybir.dt.float32
BF16 = mybir.dt.bfloat16
AF = mybir.ActivationFunctionType
ALU = mybir.AluOpType

D = 1024
KC = 8
FF = 2816
FC = 22
EPS = 1e-6
DEPTH = 4
SEQ = 8192
BATCH = 4
POOL_WINS = (2, 4, 8, 16)


def vec_layout():
    off = {}
    n = 0

    def add(name, c):
        nonlocal n
        off[name] = n
        n += c

    for l in range(DEPTH):
        add(f"gmix{l}", 8)
        add(f"gffn{l}", 8)
        for j in range(3):
            add(f"fcw{l}_{j}", FC)
        add(f"fcb{l}", FC)
    for e in range(2):
        for j in range(4):
            add(f"lcw{e}_{j}", 4)
        add(f"lcb{e}", 4)
        add(f"lba{e}", 4)
        add(f"lbi{e}", 4)
        add(f"lam{e}", 4)
        for j in range(3):
            add(f"scw{e}_{j}", 4)
    for o in range(2):
        add(f"psc{o}", 4)
        add(f"lbl{o}", 4)
        add(f"hng{o}", 4)
    return off, n


VOFF, NV = vec_layout()


def _cols(v):
    v = np.asarray(v, np.float32).reshape(-1, 128)
    return v.T


def host_pack(inp):
    vecs = np.zeros((128, NV), np.float32)

    def put(name, v):
        c = _cols(v)
        vecs[:, VOFF[name]:VOFF[name] + c.shape[1]] = c

    for l in range(DEPTH):
        put(f"gmix{l}", inp["g_mix"][l])
        put(f"gffn{l}", inp["g_ffn"][l])
        for j in range(3):
            put(f"fcw{l}_{j}", inp["ffn_conv_w"][l, j])
        put(f"fcb{l}", inp["ffn_conv_b"][l])
    for e in range(2):
        for j in range(4):
            put(f"lcw{e}_{j}", inp["lru_conv_w"][e, j])
        put(f"lcb{e}", inp["lru_conv_b"][e])
        put(f"lba{e}", inp["lru_ba"][e])
        put(f"lbi{e}", inp["lru_bi"][e])
        put(f"lam{e}", inp["lru_lambda"][e])
        for j in range(3):
            put(f"scw{e}_{j}", inp["sconv_w"][e, j])
    for o in range(2):
        put(f"psc{o}", inp["pool_scale"][o])
        put(f"lbl{o}", inp["hgrn_lb_logits"][o])
        put(f"hng{o}", inp["hgrn_norm_g"][o])
    wbd = np.zeros((2, 2, 4, 128, 128), np.float32)
    for e in range(2):
        for gi, nm in enumerate(("lru_wa", "lru_wi")):
            w = np.asarray(inp[nm][e], np.float32)
            for c in range(4):
                wbd[e, gi, c, 0:64, 0:64] = w[2 * c]
                wbd[e, gi, c, 64:128, 64:128] = w[2 * c + 1]
    gfin = np.ascontiguousarray(np.broadcast_to(np.asarray(inp["g_final"], np.float32)[None, :], (128, D)))
    ident = np.eye(128, dtype=np.float32)
    s = np.arange(128)[:, None]
    t = np.arange(128)[None, :]
    trimask = ((s <= t) & (s // 64 == t // 64)).astype(np.float32)
    cmask = np.ones((128, 512), np.float32)
    cmask[:, ::64] = 0.0
    poolinv = np.zeros((128, 4, 16), np.float32)
    for g, win in enumerate(POOL_WINS):
        poolinv[:, g, :] = 1.0 / np.minimum(np.arange(1, 17), win)[None, :]
    ones128 = np.full((128, 128), 1.0 / 128, np.float32)
    consts = np.concatenate([ident, trimask, cmask, poolinv.reshape(128, 64), ones128], axis=1)
    f = lambda k: np.ascontiguousarray(np.asarray(inp[k], np.float32))
    return {
        "vecs": vecs, "wbd": wbd, "gfin": gfin, "consts": np.ascontiguousarray(consts),
        "w_in_even": f("w_in_even"), "w_out_even": f("w_out_even"), "w_in_odd": f("w_in_odd"),
        "w_out_odd": f("w_out_odd"), "pool_w": f("pool_w"), "ffn_w_up": f("ffn_w_up"),
        "ffn_w_gate": f("ffn_w_gate"), "ffn_w_down": f("ffn_w_down"),
    }


NCONST = 128 + 128 + 512 + 64 + 128


class Tok:
    __slots__ = ("sem", "val")

    def __init__(self, sem, val):
        self.sem = sem
        self.val = val


class KeyState:
    __slots__ = ("w", "r")

    def __init__(self):
        self.w = {}
        self.r = {}


def _merge(d, tok):
    k = id(tok.sem)
    if k not in d or d[k].val < tok.val:
        d[k] = tok


class Builder:
    def __init__(self, nc, stack):
        self.nc = nc
        self.stack = stack
        self.eng = {"pe": nc.tensor, "dve": nc.vector, "act": nc.scalar, "pool": nc.gpsimd, "sp": nc.sync}
        self.sem = {}
        self.cnt = {}
        self.waited = {k: {} for k in self.eng}
        for k in self.eng:
            self.sem[k] = stack.enter_context(nc.semaphore("s_" + k))
            self.cnt[k] = 0
        self.dsems = {}
        self.dcnt = {}
        self.keys = {}
        self.ninstr = 0
        self.dead = False

    def stop_at(self, n):
        if DBG_STOP == n:
            self.dead = True

    def ks(self, k):
        s = self.keys.get(k)
        if s is None:
            s = self.keys[k] = KeyState()
        return s

    def wait(self, e, toks):
        if self.dead:
            return
        h = self.eng[e]
        for t in toks:
            k = id(t.sem)
            if self.waited[e].get(k, 0) >= t.val:
                continue
            self.waited[e][k] = t.val
            h.wait_ge(t.sem, t.val)
            self.ninstr += 1

    def _deps(self, e, r, w):
        own = id(self.sem[e]) if e in self.sem else None
        best = {}
        for k in r:
            for t in self.ks(k).w.values():
                if e == "pe" and id(t.sem) == own:
                    continue
                _merge(best, t)
        for k in w:
            st = self.ks(k)
            for t in list(st.w.values()) + list(st.r.values()):
                if id(t.sem) == own:
                    continue
                _merge(best, t)
        return list(best.values())

    def _record(self, tok, r, w):
        for k in r:
            _merge(self.ks(k).r, tok)
        for k in w:
            st = self.ks(k)
            st.w = {id(tok.sem): tok}
            st.r = {}

    def op(self, e, fn, r=(), w=(), inc=True):
        if self.dead:
            return None
        self.wait(e, self._deps(e, r, w))
        ins = fn(self.eng[e])
        self.ninstr += 1
        if inc:
            self.cnt[e] += 1
            ins.then_inc(self.sem[e], 1)
            tok = Tok(self.sem[e], self.cnt[e])
        else:
            tok = Tok(self.sem[e], self.cnt[e] + 1)
        self._record(tok, r, w)
        return tok

    def dma(self, q, slot, out, in_, r=(), w=()):
        if self.dead:
            return None
        if slot not in self.dsems:
            self.dsems[slot] = self.stack.enter_context(self.nc.semaphore("d_" + slot))
            self.dcnt[slot] = 0
        self.wait(q, self._deps("dma", r, w))
        ins = self.eng[q].dma_start(out=out, in_=in_)
        self.ninstr += 1
        self.dcnt[slot] += 16
        ins.then_inc(self.dsems[slot], 16)
        tok = Tok(self.dsems[slot], self.dcnt[slot])
        self._record(tok, r, w)
        return tok

    def mm_gen(self, out_ap, out_key, pairs, r, chunk, pair_r=None):
        n = len(pairs)
        if not self.dead:
            self.wait("pe", self._deps("pe", r, [out_key]))
        for i, (l, rh) in enumerate(pairs):
            last = (i == n - 1)
            if not self.dead:
                if pair_r is not None:
                    self.wait("pe", self._deps("pe", [pair_r[i]], ()))
                ins = self.eng["pe"].matmul(out_ap, l, rh, start=(i == 0), stop=last)
                self.ninstr += 1
                if last:
                    self.cnt["pe"] += 1
                    ins.then_inc(self.sem["pe"], 1)
                    self._record(Tok(self.sem["pe"], self.cnt["pe"]), list(r) + (list(pair_r) if pair_r is not None else []), [out_key])
            if (i + 1) % chunk == 0 and not last:
                yield

    def mm(self, out_ap, out_key, pairs, r):
        for _ in self.mm_gen(out_ap, out_key, pairs, r, 1 << 30):
            pass

    def barrier(self, force=False):
        if self.dead and not force:
            return
        self.dead = False
        toks = [Tok(self.sem[e], self.cnt[e]) for e in self.eng if self.cnt[e] > 0]
        toks += [Tok(self.dsems[s], self.dcnt[s]) for s in self.dsems if self.dcnt[s] > 0]
        for e in self.eng:
            self.wait(e, [t for t in toks if t.sem is not self.sem[e]])
        self.keys = {k: v for k, v in self.keys.items() if k.startswith("xs")}


DBG_STOP = 0
import os as _os
EVAC_DVE = bool(int(_os.environ.get('EVAC_DVE', '0')))


def run_chains(factories, nblocks):
    free = list(range(nblocks))
    active = []
    pend = list(factories)
    while pend or active:
        while pend and free:
            blk = free.pop(0)
            active.append((pend.pop(0)(blk), blk))
        for item in list(active):
            g, blk = item
            try:
                next(g)
            except StopIteration:
                active.remove(item)
                free.append(blk)


def run_chains2(groups):
    st = [{"pend": list(t[0]), "free": list(range(t[1])), "active": [], "wt": (t[2] if len(t) > 2 else 1)} for t in groups]
    while any(g["pend"] or g["active"] for g in st):
        for g in st:
            while g["pend"] and g["free"]:
                blk = g["free"].pop(0)
                g["active"].append((g["pend"].pop(0)(blk), blk))
        for g in st:
            for _ in range(g["wt"]):
                for item in list(g["active"]):
                    gen, blk = item
                    try:
                        next(gen)
                    except StopIteration:
                        g["active"].remove(item)
                        g["free"].append(blk)


class TempPool:
    def __init__(self, sbf, name, n, width, dt):
        self.bufs = [sbf(f"{name}{i}", [128, width], dt) for i in range(n)]
        self.name = name
        self.i = 0

    def get(self):
        i = self.i
        self.i = (i + 1) % len(self.bufs)
        return self.bufs[i], f"{self.name}{i}"


def build_program(S, layers, do_final, TM=512, TF=256, only_mix=False):
    nc = bass.Bass("TRN2", target_bir_lowering=False)
    dt_in = lambda name, shape: nc.dram_tensor(name, shape, F32, kind="ExternalInput").ap()
    x_in = dt_in("x", [S, D])
    w_in_even = dt_in("w_in_even", [2, D, 2560])
    w_out_even = dt_in("w_out_even", [2, D, D])
    w_in_odd = dt_in("w_in_odd", [2, D, 2560])
    w_out_odd = dt_in("w_out_odd", [2, D, D])
    pool_w = dt_in("pool_w", [2, 4, 128, 128])
    w_up = dt_in("ffn_w_up", [DEPTH, D, FF])
    w_gate = dt_in("ffn_w_gate", [DEPTH, D, FF])
    w_down = dt_in("ffn_w_down", [DEPTH, FF, D])
    wbd_d = dt_in("wbd", [2, 2, 4, 128, 128])
    vecs_d = dt_in("vecs", [128, NV])
    gfin_d = dt_in("gfin", [128, D])
    consts_d = dt_in("consts", [128, NCONST])
    out_d = nc.dram_tensor("out", [S, D], F32, kind="ExternalOutput").ap()
    xs_d = nc.dram_tensor("xs", [S, D], F32, kind="Internal").ap()

    phases = []
    for l in layers:
        phases.append(("mix", l))
        if not only_mix:
            phases.append(("ffn", l))
    if do_final:
        phases.append(("final", -1))

    with ExitStack() as top:
        B = Builder(nc, top)
        sbt = lambda name, shape, dt: top.enter_context(nc.sbuf_tensor("sb_" + name, shape, dt))
        vecs = sbt("vecs", [128, NV], F32)
        dv = sbt("dv", [128, 64], F32)
        constf = sbt("constf", [128, NCONST], F32)
        ident = sbt("ident", [128, 128], BF16)
        ones_bf = sbt("ones_bf", [128, 128], BF16)
        ss = sbt("ss", [128, 4], F32)
        rs = sbt("rs", [128, 4], F32)
        trimask = constf[:, 128:256]
        cmask = constf[:, 256:768]
        poolinv = constf[:, 768:832]
        psb = [top.enter_context(nc.psum_tensor(f"psb{i}", [128, 1024], BF16)) for i in range(2)]
        psf = [top.enter_context(nc.psum_tensor(f"psf{i}", [128, 512], F32)) for i in range(6)]

        def V(name, c=0):
            o = VOFF[name] + c
            return vecs[:, o:o + 1]

        DVOFF = {}
        _n = 0
        for e in range(2):
            for nm in ("clam", "clam2"):
                DVOFF[f"{nm}{e}"] = _n
                _n += 4
        for o in range(2):
            for nm in ("lb", "oml", "noml"):
                DVOFF[f"{nm}{o}"] = _n
                _n += 4
        DVOFF["tmp"] = _n

        def DV(name, c=0):
            o = DVOFF[name] + c
            return dv[:, o:o + 1]

        B.dma("sp", "c0", vecs[:, :], vecs_d, w=["vecs"])
        B.dma("sp", "c1", constf[:, :], consts_d, w=["constf"])
        B.op("dve", lambda e: e.tensor_copy(ident[:, :], constf[:, 0:128]), r=["constf"], w=["ident"])
        B.op("dve", lambda e: e.tensor_copy(ones_bf[:, :], constf[:, 832:960]), r=["constf"], w=["ones"])
        for e_ in range(2):
            lo = DVOFF[f"clam{e_}"]
            lo2 = DVOFF[f"clam2{e_}"]
            lam = vecs[:, VOFF[f"lam{e_}"]:VOFF[f"lam{e_}"] + 4]
            B.op("act", lambda e, lo=lo, lam=lam: e.activation(dv[:, lo:lo + 4], lam, AF.Exp, scale=-1.0), r=["vecs"], w=["dv"])
            B.op("act", lambda e, lo=lo: e.activation(dv[:, lo:lo + 4], dv[:, lo:lo + 4], AF.Ln, bias=1.0), r=["dv"], w=["dv"])
            B.op("dve", lambda e, lo=lo, lo2=lo2: e.tensor_scalar(dv[:, lo2:lo2 + 4], dv[:, lo:lo + 4], -16.0, None, ALU.mult), r=["dv"], w=["dv"])
            B.op("dve", lambda e, lo=lo: e.tensor_scalar(dv[:, lo:lo + 4], dv[:, lo:lo + 4], -8.0, None, ALU.mult), r=["dv"], w=["dv"])
        l0 = DVOFF["lb0"]
        B.op("dve", lambda e: e.memset(dv[:, l0:l0 + 4], 0.0), w=["dv"])
        B.op("dve", lambda e: e.memset(dv[:, DVOFF["oml0"]:DVOFF["oml0"] + 4], 1.0), w=["dv"])
        B.op("dve", lambda e: e.memset(dv[:, DVOFF["noml0"]:DVOFF["noml0"] + 4], -1.0), w=["dv"])
        l1 = DVOFF["lb1"]
        tmpo = DVOFF["tmp"]
        B.op("dve", lambda e: e.tensor_tensor(dv[:, tmpo:tmpo + 4], vecs[:, VOFF["lbl1"]:VOFF["lbl1"] + 4],
                                              vecs[:, VOFF["lbl0"]:VOFF["lbl0"] + 4], ALU.subtract), r=["vecs", "dv"], w=["dv"])
        B.op("act", lambda e: e.activation(dv[:, l1:l1 + 4], dv[:, tmpo:tmpo + 4], AF.Sigmoid), r=["dv"], w=["dv"])
        B.op("dve", lambda e: e.tensor_scalar(dv[:, DVOFF["oml1"]:DVOFF["oml1"] + 4], dv[:, l1:l1 + 4], -1.0, 1.0, ALU.mult, ALU.add), r=["dv"], w=["dv"])
        B.op("dve", lambda e: e.tensor_scalar(dv[:, DVOFF["noml1"]:DVOFF["noml1"] + 4], dv[:, l1:l1 + 4], -1.0, None, ALU.add), r=["dv"], w=["dv"])
        B.barrier()
        B.stop_at(1)

        class Ctx:
            pass

        def rot(lst, name):
            st = {"i": 0}

            def nxt():
                i = st["i"]
                st["i"] = (i + 1) % len(lst)
                return lst[i], f"{name}{i}"
            return nxt

        def load_x(C, src, ti, T):
            NS = T // 128
            b = ti % 2
            tok0 = ti * T
            xk = [f"xs{j}" for j in range(tok0 // 256, (tok0 + T) // 256)]
            B.dma("sp", f"ldx{b}", C.xt[b][:, 0:NS, :], src[tok0:tok0 + T, :].rearrange("(s p) d -> p s d", p=128),
                  r=xk, w=[f"xt{b}"])

        def normA_gen(C, ti, T):
            NS = T // 128
            b = ti % 2
            xt = C.xt[b]
            xk = f"xt{b}"
            for s in range(NS):
                B.op("act", lambda e, s=s: e.activation(C.hb16[:, s, :], xt[:, s, :], AF.Square, accum_out=ss[:, s:s + 1]), r=[xk], w=["ss", f"hb{s}"])
                yield
            B.op("dve", lambda e: e.tensor_scalar(rs[:, 0:NS], ss[:, 0:NS], 1.0 / D, EPS, ALU.mult, ALU.add), r=["ss"], w=["rs"])
            B.op("act", lambda e: e.activation(rs[:, 0:NS], rs[:, 0:NS], AF.Sqrt), r=["rs"], w=["rs"])
            yield
            B.op("dve", lambda e: e.reciprocal(rs[:, 0:NS], rs[:, 0:NS]), r=["rs"], w=["rs"])
            yield
            for s in range(NS):
                if s % 2 == 0:
                    B.op("dve", lambda e, s=s: e.tensor_scalar(C.hb16[:, s, :], xt[:, s, :], rs[:, s:s + 1], None, ALU.mult),
                         r=[xk, "rs"], w=[f"hb{s}"])
                else:
                    B.op("act", lambda e, s=s: e.activation(C.hb16[:, s, :], xt[:, s, :], AF.Identity, scale=rs[:, s:s + 1]),
                         r=[xk, "rs"], w=[f"hb{s}"])
                yield

        def normB_gen(C, ti, T, gname):
            NS = T // 128
            b = ti % 2
            hT = C.hT[b]
            for kc in range(KC):
                reg, rk = C.ptr()
                for s in range(NS):
                    B.op("pe", lambda e, s=s, kc=kc, reg=reg: e.transpose(reg[:, s * 128:(s + 1) * 128], C.hb16[:, s, kc * 128:(kc + 1) * 128], ident[:, :]),
                         r=[f"hb{s}", "ident"] if s == 0 else [f"hb{s}"], w=[rk] if s == 0 else (), inc=(s == NS - 1))
                g = V(gname, kc)
                if kc % 2 == 0 or EVAC_DVE:
                    B.op("dve", lambda e, kc=kc, reg=reg, g=g: e.tensor_scalar(hT[:, kc, 0:T], reg[:, 0:T], g, None, ALU.mult),
                         r=[rk, "vecs"], w=[f"hT{b}_{kc}"])
                else:
                    B.op("act", lambda e, kc=kc, reg=reg, g=g: e.activation(hT[:, kc, 0:T], reg[:, 0:T], AF.Identity, scale=g),
                         r=[rk, "vecs"], w=[f"hT{b}_{kc}"])
                yield

        def run_gen(g):
            for _ in g:
                pass

        def interleave(gens):
            gens = list(gens)
            while gens:
                for g in list(gens):
                    try:
                        next(g)
                    except StopIteration:
                        gens.remove(g)

        def tile_begin(C, src, ti, T, NT, gname):
            if ti == 0:
                C.oproj = {}
                load_x(C, src, 0, T)
                run_gen(normA_gen(C, 0, T))
                run_gen(normB_gen(C, 0, T, gname))
                if NT > 1:
                    load_x(C, src, 1, T)
            if ti + 2 < NT:
                load_x(C, src, ti + 2, T)
            C.next_normA = (lambda blk=None: normA_gen(C, ti + 1, T)) if ti + 1 < NT else None

        def tile_end(C, ti, T, NT, gname, oproj_factory, depth):
            C.oproj[ti] = oproj_factory
            gens = []
            if ti + 1 < NT:
                gens.append(normB_gen(C, ti + 1, T, gname))
            if ti - depth >= 0:
                gens.append(C.oproj.pop(ti - depth)())
            interleave(gens)

        def tile_finish(C, NT):
            for j in sorted(C.oproj):
                run_gen(C.oproj[j]())
            C.oproj = {}

        def hT_keys(b):
            return [f"hT{b}_{kc}" for kc in range(KC)]

        pending = []

        def defer(fn):
            if pending:
                pending.pop()()
            pending.append(fn)

        def flush():
            while pending:
                pending.pop()()

        xr_state = {"i": 0}

        def out_proj_gen(C, src, dst, ti, T, yT, ykeys, nk, wt, wkeys_for_half, chunk):
            NS = T // 128
            tok0 = ti * T
            for s in range(NS):
                j = xr_state["i"]
                xr_state["i"] = (j + 1) % len(C.xr)
                xr = C.xr[j]
                xrk = f"xr{j}"
                t0 = tok0 + s * 128
                xk = [f"xs{t0 // 256}"]
                B.dma("sp", f"ldr{j}", xr[:, :], src[t0:t0 + 128, :], r=xk, w=[xrk])
                for half in range(2):
                    pb, pk = C.gp()
                    yield from B.mm_gen(pb[:, :], pk, [(yT[:, kc, s * 128:(s + 1) * 128], wt[:, kc, half * 512:(half + 1) * 512]) for kc in range(nk)],
                                        list(wkeys_for_half(half)), chunk, pair_r=list(ykeys))
                    B.op("dve", lambda e, half=half, pb=pb, xr=xr: e.tensor_tensor(xr[:, half * 512:(half + 1) * 512], pb[:, :],
                                                                                  xr[:, half * 512:(half + 1) * 512], ALU.add),
                         r=[pk, xrk], w=[xrk])
                    yield
                B.dma("pool", f"str{j}", dst[t0:t0 + 128, :], xr[:, :], r=[xrk], w=xk)

        def load_w(dst_sb, src2d, rows_per_part, colblocks, keybase):
            nkc = rows_per_part
            for j, (c0, c1) in enumerate(colblocks):
                B.dma("pool", f"{keybase}{j}", dst_sb[:, 0:nkc, c0:c1], src2d[:, c0:c1].rearrange("(kc p) n -> p kc n", p=128),
                      w=[f"{keybase}{j}"])

        src = x_in
        for pi, (kind, l) in enumerate(phases):
            last = pi == len(phases) - 1
            dst = out_d if last else xs_d
            with ExitStack() as ph:
                sbf = lambda name, shape, dt, ph=ph, pi=pi: ph.enter_context(nc.sbuf_tensor(f"p{pi}_{name}", shape, dt))
                C = Ctx()
                C.ptr = rot([psb[0][:, 0:512], psb[1][:, 0:512]], "ptr")
                if kind == "final":
                    T = TM
                    NS = T // 128
                    C.xt = [sbf(f"xt{i}", [128, NS, D], F32) for i in range(2)]
                    C.junk = sbf("junk", [128, D], BF16)
                    gfin = sbf("gfin", [128, D], F32)
                    B.dma("sp", "c2", gfin[:, :], gfin_d, w=["gfin"])
                    NT = S // T
                    load_x(C, src, 0, T)
                    for ti in range(NT):
                        b = ti % 2
                        if ti + 1 < NT:
                            load_x(C, src, ti + 1, T)
                        xt = C.xt[b]
                        xk = f"xt{b}"
                        for s in range(NS):
                            B.op("act", lambda e, s=s: e.activation(C.junk[:, :], xt[:, s, :], AF.Square, accum_out=ss[:, s:s + 1]), r=[xk], w=["ss"])
                        B.op("dve", lambda e: e.tensor_scalar(rs[:, 0:NS], ss[:, 0:NS], 1.0 / D, EPS, ALU.mult, ALU.add), r=["ss"], w=["rs"])
                        B.op("act", lambda e: e.activation(rs[:, 0:NS], rs[:, 0:NS], AF.Sqrt), r=["rs"], w=["rs"])
                        B.op("dve", lambda e: e.reciprocal(rs[:, 0:NS], rs[:, 0:NS]), r=["rs"], w=["rs"])
                        for s in range(NS):
                            B.op("dve", lambda e, s=s: e.scalar_tensor_tensor(xt[:, s, :], xt[:, s, :], rs[:, s:s + 1], gfin[:, :], ALU.mult, ALU.mult),
                                 r=[xk, "rs", "gfin"], w=[xk])
                        tok0 = ti * T
                        xkeys = [f"xs{j}" for j in range(tok0 // 256, (tok0 + T) // 256)]
                        B.dma("pool", f"stx{b}", dst[tok0:tok0 + T, :].rearrange("(s p) d -> p s d", p=128), xt[:, 0:NS, :], r=[xk], w=xkeys)

                elif kind == "ffn":
                    T = TF
                    NS = T // 128
                    NT = S // T
                    wu = sbf("wu", [128, KC, FF], BF16)
                    wg = sbf("wg", [128, KC, FF], BF16)
                    wd = sbf("wd", [128, FC, D], BF16)
                    half = FF // 2
                    load_w(wu, w_up[l], KC, [(0, 512), (512, half), (half, FF)], "wu")
                    load_w(wg, w_gate[l], KC, [(0, 512), (512, half), (half, FF)], "wg")
                    for j, (k0, k1) in enumerate([(0, 11), (11, 22)]):
                        B.dma("pool", f"wd{j}", wd[:, k0:k1, :], w_down[l][k0 * 128:k1 * 128, :].rearrange("(kc p) n -> p kc n", p=128), w=[f"wd{j}"])

                    def ukey(base, oc):
                        c = oc * 128
                        return f"{base}{0 if c < 512 else (1 if c < half else 2)}"
                    C.xt = [sbf(f"xt{i}", [128, NS, D], F32) for i in range(2)]
                    C.xr = [sbf(f"xr{i}", [128, D], F32) for i in range(2)]
                    C.hb16 = sbf("hb16", [128, NS, D], BF16)
                    C.hT = [sbf(f"hT{i}", [128, KC, T], BF16) for i in range(2)]
                    vTs = [sbf(f"vT{i}", [128, FC, T], BF16) for i in range(2)]
                    uh = sbf("uh", [128, FC, 2], F32)
                    tp = TempPool(sbf, "tf", 6, T + 2, F32)
                    C.gp = rot([p for p in psf], "gp")
                    B.op("pool", lambda e: e.memset(uh[:, :, :], 0.0), w=["uh"])
                    for ti in range(NT):
                        b = ti % 2
                        tile_begin(C, src, ti, T, NT, f"gffn{l}")
                        hT = C.hT[b]
                        vT = vTs[b]
                        def ffn_chain(oc, blk, b=b, hT=hT, vT=vT):
                            pu, puk = C.gp()
                            B.mm(pu[:, 0:T], puk, [(wu[:, kc, oc * 128:(oc + 1) * 128], hT[:, kc, 0:T]) for kc in range(KC)],
                                 r=hT_keys(b) + [ukey("wu", oc)])
                            yield
                            pg, pgk = C.gp()
                            B.mm(pg[:, 0:T], pgk, [(wg[:, kc, oc * 128:(oc + 1) * 128], hT[:, kc, 0:T]) for kc in range(KC)],
                                 r=hT_keys(b) + [ukey("wg", oc)])
                            yield
                            ue, uek = tp.bufs[2 * blk], f"tf{2 * blk}"
                            cv, cvk = tp.bufs[2 * blk + 1], f"tf{2 * blk + 1}"
                            B.op("pool", lambda e: e.tensor_copy(ue[:, 0:2], uh[:, oc, :]), r=["uh"], w=[uek + "h"])
                            B.op("act", lambda e: e.activation(ue[:, 2:2 + T], pu[:, 0:T], AF.Copy), r=[puk], w=[uek])
                            yield
                            B.op("pool", lambda e: e.tensor_copy(uh[:, oc, :], ue[:, T:T + 2]), r=[uek], w=["uh"])
                            B.op("dve", lambda e: e.tensor_scalar(cv[:, 0:T], ue[:, 2:2 + T], V(f"fcw{l}_2", oc), V(f"fcb{l}", oc), ALU.mult, ALU.add),
                                 r=[uek, "vecs"], w=[cvk])
                            yield
                            B.op("dve", lambda e: e.scalar_tensor_tensor(cv[:, 0:T], ue[:, 1:1 + T], V(f"fcw{l}_1", oc), cv[:, 0:T], ALU.mult, ALU.add),
                                 r=[uek, uek + "h", cvk], w=[cvk])
                            yield
                            B.op("dve", lambda e: e.scalar_tensor_tensor(cv[:, 0:T], ue[:, 0:T], V(f"fcw{l}_0", oc), cv[:, 0:T], ALU.mult, ALU.add),
                                 r=[uek, uek + "h", cvk], w=[cvk])
                            yield
                            B.op("act", lambda e: e.activation(cv[:, 0:T], cv[:, 0:T], AF.Gelu_apprx_tanh), r=[cvk], w=[cvk])
                            yield
                            B.op("dve", lambda e: e.tensor_tensor(vT[:, oc, :], pg[:, 0:T], cv[:, 0:T], ALU.mult),
                                 r=[pgk, cvk], w=[f"vT{b}_{oc}"])
                            yield

                        facs = [(lambda blk, oc=oc: ffn_chain(oc, blk)) for oc in range(FC)]
                        grp = [(facs, 3)]
                        if C.next_normA is not None:
                            grp.append(([C.next_normA], 1))
                        run_chains2(grp)
                        tile_end(C, ti, T, NT, f"gffn{l}",
                                 (lambda ti=ti, vT=vT, b=b: out_proj_gen(C, src, dst, ti, T, vT, [f"vT{b}_{oc}" for oc in range(FC)], FC, wd,
                                                                         lambda hf: ["wd0", "wd1"], 11)), 0)
                    tile_finish(C, NT)

                else:
                    T = TM
                    NS = T // 128
                    NT = S // T
                    even = (l % 2 == 0)
                    e_ = l // 2
                    wi = sbf("wi", [128, KC, 2560], BF16)
                    wo = sbf("wo", [128, KC, D], BF16)
                    load_w(wi, (w_in_even if even else w_in_odd)[e_], KC, [(j * 512, (j + 1) * 512) for j in range(5)], "wi")
                    load_w(wo, (w_out_even if even else w_out_odd)[e_], KC, [(0, 512), (512, 1024)], "wo")
                    B.stop_at(2)
                    C.xt = [sbf(f"xt{i}", [128, NS, D], F32) for i in range(2)]
                    C.xr = [sbf(f"xr{i}", [128, D], F32) for i in range(2)]
                    C.hb16 = sbf("hb16", [128, NS, D], BF16)
                    C.hT = [sbf(f"hT{i}", [128, KC, T], BF16) for i in range(2)]
                    yTs = [sbf(f"yT{i}", [128, KC, T], BF16) for i in range(2)]

                    def proj(oc, hb_, gp):
                        pb, pk = gp()
                        B.mm(pb[:, 0:T], pk, [(wi[:, kc, oc * 128:(oc + 1) * 128], C.hT[hb_][:, kc, 0:T]) for kc in range(KC)],
                             r=hT_keys(hb_) + [f"wi{oc // 4}"])
                        return pb, pk

                    if even:
                        wbd = sbf("wbd", [128, 2, 4, 128], BF16)
                        for gi in range(2):
                            B.dma("pool", f"wbd{gi}", wbd[:, gi, :, :], wbd_d[e_, gi].rearrange("c p j -> p c j"), w=[f"wbd{gi}"])
                        xa = [sbf(f"xa{c}", [128, T + 3], F32) for c in range(4)]
                        pe_ = [sbf(f"pe{c}", [128, T + 2], F32) for c in range(4)]
                        gg = [sbf(f"gg{c}", [128, T], F32) for c in range(4)]
                        hst = sbf("hst", [128, 4], F32)
                        tp = TempPool(sbf, "tm", 6, T, F32)
                        tp2 = TempPool(sbf, "tn", 2, T, F32)
                        tps = TempPool(sbf, "ts", 2, T, F32)
                        tpb = TempPool(sbf, "tb", 2, T, BF16)
                        C.gp = rot([p for p in psf], "gp")
                        for c in range(4):
                            B.op("pool", lambda e, c=c: e.memset(xa[c][:, 0:3], 0.0), w=[f"xa{c}h"])
                            B.op("pool", lambda e, c=c: e.memset(pe_[c][:, 0:2], 0.0), w=[f"pe{c}h"])
                        for ti in range(NT):
                            b = ti % 2
                            tile_begin(C, src, ti, T, NT, f"gmix{l}")
                            yT = yTs[b]
                            def lru_chain(c, blk, b=b, yT=yT, ti=ti):
                                xc, xck = tp.bufs[3 * blk], f"tm{3 * blk}"
                                rt, rtk = tp.bufs[3 * blk + 1], f"tm{3 * blk + 1}"
                                at, atk = tp.bufs[3 * blk + 2], f"tm{3 * blk + 2}"
                                ig, igk = tp2.bufs[blk], f"tn{blk}"
                                xcb, xcbk = tpb.bufs[blk], f"tb{blk}"
                                xak = f"xa{c}"
                                pb, pk = proj(c, b, C.gp)
                                B.op("act", lambda e: e.activation(xa[c][:, 3:3 + T], pb[:, 0:T], AF.Copy), r=[pk], w=[xak])
                                yield
                                pb2, pk2 = proj(4 + c, b, C.gp)
                                B.op("act", lambda e: e.activation(gg[c][:, :], pb2[:, 0:T], AF.Gelu_apprx_tanh), r=[pk2], w=[f"gg{c}"])
                                yield
                                B.op("dve", lambda e: e.tensor_scalar(xc[:, :], xa[c][:, 3:3 + T], V(f"lcw{e_}_3", c), V(f"lcb{e_}", c), ALU.mult, ALU.add),
                                     r=[xak, "vecs"], w=[xck])
                                yield
                                for j in (2, 1, 0):
                                    B.op("dve", lambda e, j=j: e.scalar_tensor_tensor(xc[:, :], xa[c][:, j:j + T], V(f"lcw{e_}_{j}", c), xc[:, :], ALU.mult, ALU.add),
                                         r=[xak, xak + "h", xck], w=[xck])
                                    yield
                                B.op("pool", lambda e: e.tensor_copy(xa[c][:, 0:3], xa[c][:, T:T + 3]), r=[xak], w=[xak + "h"])
                                B.op("act", lambda e: e.activation(xcb[:, :], xc[:, :], AF.Copy), r=[xck], w=[xcbk])
                                yield
                                pr, prk = C.gp()
                                B.mm(pr[:, 0:T], prk, [(wbd[:, 0, c, :], xcb[:, :])], r=[xcbk, "wbd0"])
                                pi_, pik = C.gp()
                                B.mm(pi_[:, 0:T], pik, [(wbd[:, 1, c, :], xcb[:, :])], r=[xcbk, "wbd1"])
                                yield
                                B.op("act", lambda e: e.activation(rt[:, :], pr[:, 0:T], AF.Sigmoid, bias=V(f"lba{e_}", c)), r=[prk, "vecs"], w=[rtk])
                                yield
                                B.op("act", lambda e: e.activation(ig[:, :], pi_[:, 0:T], AF.Sigmoid, bias=V(f"lbi{e_}", c)), r=[pik, "vecs"], w=[igk])
                                yield
                                B.op("dve", lambda e: e.tensor_tensor(xc[:, :], ig[:, :], xc[:, :], ALU.mult), r=[igk, xck], w=[xck])
                                yield
                                B.op("act", lambda e: e.activation(at[:, :], rt[:, :], AF.Exp, scale=DV(f"clam{e_}", c)), r=[rtk, "dv"], w=[atk])
                                yield
                                B.op("act", lambda e: e.activation(rt[:, :], rt[:, :], AF.Exp, scale=DV(f"clam2{e_}", c)), r=[rtk, "dv"], w=[rtk])
                                yield
                                B.op("act", lambda e: e.activation(rt[:, :], rt[:, :], AF.Sqrt, bias=1.0, scale=-1.0), r=[rtk], w=[rtk])
                                yield
                                if ti == 0:
                                    B.op("dve", lambda e: e.memset(rt[:, 0:1], 1.0), r=[rtk], w=[rtk])
                                B.op("dve", lambda e: e.tensor_tensor(xc[:, :], rt[:, :], xc[:, :], ALU.mult), r=[rtk, xck], w=[xck])
                                yield
                                init = 0.0 if ti == 0 else hst[:, c:c + 1]
                                B.op("dve", lambda e: e.tensor_tensor_scan(rt[:, :], at[:, :], xc[:, :], init, ALU.mult, ALU.add),
                                     r=[atk, xck, f"hst{c}"], w=[rtk])
                                yield
                                B.op("dve", lambda e: e.tensor_copy(hst[:, c:c + 1], rt[:, T - 1:T]), r=[rtk], w=[f"hst{c}"])
                                B.op("dve", lambda e: e.tensor_tensor(yT[:, c, :], rt[:, :], gg[c][:, :], ALU.mult), r=[rtk, f"gg{c}"], w=[f"yT{b}_{c}"])
                                yield

                            def sconv_chain(c, blk, b=b, yT=yT):
                                hbx, hbk = tps.bufs[2 * blk], f"ts{2 * blk}"
                                cv, cvk = tps.bufs[2 * blk + 1], f"ts{2 * blk + 1}"
                                pek = f"pe{c}"
                                pb, pk = proj(8 + c, b, C.gp)
                                B.op("act", lambda e: e.activation(hbx[:, :], pb[:, 0:T], AF.Copy), r=[pk], w=[hbk])
                                yield
                                pb2, pk2 = proj(16 + c, b, C.gp)
                                B.op("dve", lambda e: e.tensor_tensor(pe_[c][:, 2:2 + T], pb2[:, 0:T], hbx[:, :], ALU.mult), r=[pk2, hbk], w=[pek])
                                yield
                                B.op("dve", lambda e: e.tensor_scalar(cv[:, :], pe_[c][:, 2:2 + T], V(f"scw{e_}_2", c), None, ALU.mult), r=[pek, "vecs"], w=[cvk])
                                yield
                                B.op("dve", lambda e: e.scalar_tensor_tensor(cv[:, :], pe_[c][:, 1:1 + T], V(f"scw{e_}_1", c), cv[:, :], ALU.mult, ALU.add), r=[pek, pek + "h", cvk], w=[cvk])
                                yield
                                B.op("dve", lambda e: e.scalar_tensor_tensor(cv[:, :], pe_[c][:, 0:T], V(f"scw{e_}_0", c), cv[:, :], ALU.mult, ALU.add), r=[pek, pek + "h", cvk], w=[cvk])
                                yield
                                B.op("pool", lambda e: e.tensor_copy(pe_[c][:, 0:2], pe_[c][:, T:T + 2]), r=[pek], w=[pek + "h"])
                                pb3, pk3 = proj(12 + c, b, C.gp)
                                B.op("dve", lambda e: e.tensor_tensor(yT[:, 4 + c, :], pb3[:, 0:T], cv[:, :], ALU.mult), r=[pk3, cvk], w=[f"yT{b}_{4 + c}"])
                                yield

                            lru_f = [(lambda blk, c=c: lru_chain(c, blk)) for c in range(4)]
                            sc_f = [(lambda blk, c=c: sconv_chain(c, blk)) for c in range(4)]
                            grp = [(lru_f, 2), (sc_f, 1)]
                            if C.next_normA is not None:
                                grp.append(([C.next_normA], 1))
                            run_chains2(grp)
                            tile_end(C, ti, T, NT, f"gmix{l}",
                                     (lambda ti=ti, yT=yT, b=b: out_proj_gen(C, src, dst, ti, T, yT, [f"yT{b}_{c}" for c in range(KC)], KC, wo, lambda hf: [f"wo{hf}"], 8)), 1)
                        tile_finish(C, NT)

                    else:
                        o_ = e_
                        pw = sbf("pw", [128, 4, 128], BF16)
                        B.dma("pool", "pw", pw[:, :, :], pool_w[o_].rearrange("g p j -> p g j"), w=["pw"])
                        ue = [sbf(f"ue{c}", [128, T + 16], F32) for c in range(4)]
                        Vt = sbf("Vt", [128, NS, 512], BF16)
                        Sst = sbf("Sst", [128, 4, 128], F32)
                        Sb = [sbf(f"Sb{i}", [128, 4, 128], BF16) for i in range(2)]
                        tp = TempPool(sbf, "tm", 16, T + 16, F32)
                        tpb = TempPool(sbf, "tb", 10, T, BF16)
                        tam = TempPool(sbf, "am", 4, 128, BF16)
                        gpl = rot([psf[0], psf[1]], "gp")
                        C.gp = gpl
                        pa_r = rot([psf[2][:, i * 128:(i + 1) * 128] for i in range(4)], "pa")
                        pd_r = rot([psf[3][:, i * 128:(i + 1) * 128] for i in range(4)], "pd")
                        po = [psf[4], psf[5]]
                        for c in range(4):
                            B.op("pool", lambda e, c=c: e.memset(ue[c][:, 0:16], 0.0), w=[f"ue{c}h"])
                        B.op("pool", lambda e: e.memset(Sst[:, :, :], 0.0), w=[f"S{h}" for h in range(4)])
                        B.op("pool", lambda e: e.memset(Sb[0][:, :, :], 0.0), w=[f"Sb0_{h}" for h in range(4)])
                        sbpar = [0, 0, 0, 0]
                        for ti in range(NT):
                            b = ti % 2
                            tile_begin(C, src, ti, T, NT, f"gmix{l}")
                            yT = yTs[b]
                            def pool_chain(c, blk, b=b, yT=yT, ti=ti):
                                win = POOL_WINS[c]
                                uk = f"ue{c}"
                                W = T + 16
                                tb_ = [(tp.bufs[12 + 2 * blk + k], f"tm{12 + 2 * blk + k}") for k in range(2)]
                                pp, ppk = tpb.bufs[8 + blk], f"tb{8 + blk}"
                                pb, pk = proj(c, b, gpl)
                                B.op("act", lambda e: e.activation(ue[c][:, 16:16 + T], pb[:, 0:T], AF.Copy), r=[pk], w=[uk])
                                yield
                                prev, prevk = ue[c], uk
                                first = True
                                step = 1
                                k = 0
                                while step < win:
                                    nx, nxk = tb_[k % 2]
                                    k += 1
                                    B.op("dve", lambda e, nx=nx, prev=prev, step=step: e.tensor_tensor(nx[:, 2 * step - 1:W], prev[:, 2 * step - 1:W], prev[:, step - 1:W - step], ALU.add),
                                         r=[prevk, uk + "h"] if first else [prevk], w=[nxk])
                                    yield
                                    first = False
                                    prev, prevk = nx, nxk
                                    step *= 2
                                if ti == 0:
                                    t16, t16k = tb_[k % 2]
                                    B.op("dve", lambda e: e.tensor_tensor(t16[:, 0:16], prev[:, 16:32], poolinv[:, c * 16:(c + 1) * 16], ALU.mult),
                                         r=[prevk, "constf"], w=[t16k])
                                    B.op("dve", lambda e: e.tensor_tensor(pp[:, 0:16], t16[:, 0:16], ue[c][:, 16:32], ALU.subtract),
                                         r=[t16k, uk], w=[ppk])
                                    B.op("dve", lambda e: e.scalar_tensor_tensor(pp[:, 16:T], prev[:, 32:16 + T], 1.0 / win, ue[c][:, 32:16 + T], ALU.mult, ALU.subtract),
                                         r=[prevk, uk, ppk], w=[ppk])
                                else:
                                    B.op("dve", lambda e: e.scalar_tensor_tensor(pp[:, 0:T], prev[:, 16:16 + T], 1.0 / win, ue[c][:, 16:16 + T], ALU.mult, ALU.subtract),
                                         r=[prevk, uk], w=[ppk])
                                yield
                                B.op("pool", lambda e: e.tensor_copy(ue[c][:, 0:16], ue[c][:, T:T + 16]), r=[uk], w=[uk + "h"])
                                py, pyk = gpl()
                                B.mm(py[:, 0:T], pyk, [(pw[:, c, :], pp[:, 0:T])], r=[ppk, "pw"])
                                B.op("act", lambda e: e.activation(yT[:, c, :], py[:, 0:T], AF.Identity, scale=V(f"psc{o_}", c)), r=[pyk, "vecs"], w=[f"yT{b}_{c}"])
                                yield

                            def head_prep(hd, blk, h, b=b):
                                fb = lambda k: (tp.bufs[6 * blk + k], f"tm{6 * blk + k}")
                                bb = lambda k: (tpb.bufs[4 * blk + k], f"tb{4 * blk + k}")
                                h.q, h.qk = fb(0)
                                h.sg, h.sgk = fb(1)
                                h.lf, h.lfk = fb(2)
                                h.G, h.Gk = fb(3)
                                h.Eg, h.Egk = fb(4)
                                h.sd, h.sdk = fb(5)
                                h.Qt, h.Qtk = bb(0)
                                h.Kt, h.Ktk = bb(1)
                                h.Kh, h.Khk = bb(2)
                                h.KhT, h.KhTk = bb(3)
                                pb, pk = proj(4 + hd, b, gpl)
                                B.op("act", lambda e: e.activation(h.q[:, 0:T], pb[:, 0:T], AF.Copy), r=[pk], w=[h.qk])
                                yield
                                pb2, pk2 = proj(8 + hd, b, gpl)
                                B.op("act", lambda e: e.activation(h.sg[:, 0:T], pb2[:, 0:T], AF.Sigmoid), r=[pk2], w=[h.sgk])
                                yield
                                B.op("dve", lambda e: e.tensor_scalar(h.lf[:, 0:T], h.sg[:, 0:T], DV(f"oml{o_}", hd), DV(f"lb{o_}", hd), ALU.mult, ALU.add),
                                     r=[h.sgk, "dv"], w=[h.lfk])
                                yield
                                B.op("act", lambda e: e.activation(h.lf[:, 0:T], h.lf[:, 0:T], AF.Ln), r=[h.lfk], w=[h.lfk])
                                B.op("dve", lambda e: e.tensor_scalar(h.sg[:, 0:T], h.sg[:, 0:T], DV(f"noml{o_}", hd), DV(f"oml{o_}", hd), ALU.mult, ALU.add),
                                     r=[h.sgk, "dv"], w=[h.sgk])
                                yield
                                B.op("dve", lambda e: e.tensor_tensor_scan(h.G[:, 0:T], cmask[:, 0:T], h.lf[:, 0:T], 0.0, ALU.mult, ALU.add),
                                     r=[h.lfk, "constf"], w=[h.Gk])
                                yield
                                B.op("act", lambda e: e.activation(h.Eg[:, 0:T], h.G[:, 0:T], AF.Exp), r=[h.Gk], w=[h.Egk])
                                yield
                                B.op("act", lambda e: e.activation(h.G[:, 0:T], h.G[:, 0:T], AF.Exp, scale=-1.0), r=[h.Gk], w=[h.Gk])
                                B.op("dve", lambda e: e.tensor_tensor(h.Qt[:, 0:T], h.q[:, 0:T], h.Eg[:, 0:T], ALU.mult), r=[h.qk, h.Egk], w=[h.Qtk])
                                yield
                                B.op("dve", lambda e: e.tensor_tensor(h.sg[:, 0:T], h.sg[:, 0:T], h.G[:, 0:T], ALU.mult), r=[h.sgk, h.Gk], w=[h.sgk])
                                yield
                                B.op("act", lambda e: e.activation(h.Kt[:, 0:T], h.sg[:, 0:T], AF.Copy), r=[h.sgk], w=[h.Ktk])
                                B.op("dve", lambda e: e.tensor_tensor(
                                    h.Kh[:, 0:T].rearrange("p (c l) -> p c l", l=64), h.sg[:, 0:T].rearrange("p (c l) -> p c l", l=64),
                                    h.Eg[:, 0:T].rearrange("p (c l) -> p c l", l=64)[:, :, 63:64].to_broadcast([128, T // 64, 64]), ALU.mult),
                                    r=[h.sgk, h.Egk], w=[h.Khk])
                                yield
                                pb3, pk3 = proj(16 + hd, b, gpl)
                                B.op("act", lambda e: e.activation(h.sd[:, 0:T], pb3[:, 0:T], AF.Silu), r=[pk3], w=[h.sdk])
                                yield
                                reg, rk = C.ptr()
                                for s in range(NS):
                                    B.op("pe", lambda e, s=s: e.transpose(reg[:, s * 128:(s + 1) * 128], h.Kh[:, s * 128:(s + 1) * 128], ident[:, :]),
                                         r=[h.Khk, "ident"] if s == 0 else (), w=[rk] if s == 0 else (), inc=(s == NS - 1))
                                B.op("act", lambda e: e.activation(h.KhT[:, 0:T], reg[:, 0:T], AF.Copy), r=[rk], w=[h.KhTk])
                                yield

                            for s in range(NS):
                                pv, pvk = gpl()
                                B.mm(pv[:, :], pvk, [(C.hT[b][:, kc, s * 128:(s + 1) * 128], wi[:, kc, 1536:2048]) for kc in range(KC)],
                                     r=hT_keys(b) + ["wi3"])
                                B.op("act", lambda e, s=s, pv=pv: e.activation(Vt[:, s, :], pv[:, :], AF.Copy), r=[pvk], w=[f"Vt{s}"])
                            def head_chain(hd, blk, b=b, yT=yT):
                                h = Ctx()
                                yield from head_prep(hd, blk, h)
                                pob = po[blk]
                                pok = f"po{blk}"
                                hbank = psf[2 + blk]
                                hbk = f"pab{blk}"
                                for dc in range(NS):
                                    sl = slice(dc * 128, (dc + 1) * 128)
                                    pa, pak = hbank[:, 0:128], hbk
                                    B.mm(pa, pak, [(h.Kt[:, sl], h.Qt[:, sl])], r=[h.Ktk, h.Qtk])
                                    yield
                                    am, amk = tam.get()
                                    B.op("dve", lambda e: e.tensor_tensor(am[:, :], pa, trimask, ALU.mult), r=[pak, "constf"], w=[amk])
                                    yield
                                    B.op("pe", lambda e: e.matmul(pob[:, sl], Vt[:, dc, hd * 128:(hd + 1) * 128], am[:, :], start=True, stop=False),
                                         r=[f"Vt{dc}", amk], w=[pok], inc=False)
                                    for c2 in range(2):
                                        ch = dc * 2 + c2
                                        par = sbpar[hd]
                                        csl = slice(ch * 64, (ch + 1) * 64)
                                        B.op("pe", lambda e: e.matmul(pob[:, csl], Sb[par][:, hd, :], h.Qt[:, csl], start=False, stop=(c2 == 1)),
                                             r=[f"Sb{par}_{hd}", h.Qtk], w=[pok], inc=True)
                                        if c2 == 0:
                                            yield
                                        pd, pdk = hbank[:, 128:256], hbk
                                        rows = slice(c2 * 64, (c2 + 1) * 64)
                                        B.mm(pd, pdk, [(h.KhT[rows, dc * 128:(dc + 1) * 128], Vt[rows, dc, hd * 128:(hd + 1) * 128])], r=[h.KhTk, f"Vt{dc}"])
                                        yield
                                        dcol = ch * 64 + 63
                                        B.op("dve", lambda e: e.scalar_tensor_tensor(Sst[:, hd, :], Sst[:, hd, :], h.Eg[:, dcol:dcol + 1], pd, ALU.mult, ALU.add),
                                             r=[pdk, h.Egk, f"S{hd}"], w=[f"S{hd}"])
                                        yield
                                        npar = 1 - par
                                        B.op("act", lambda e: e.activation(Sb[npar][:, hd, :], Sst[:, hd, :], AF.Copy), r=[f"S{hd}"], w=[f"Sb{npar}_{hd}"])
                                        sbpar[hd] = npar
                                        yield
                                osq, osqk = h.q, h.qk
                                B.op("act", lambda e: e.activation(osq[:, 0:T], pob[:, 0:T], AF.Square), r=[pok], w=[osqk])
                                yield
                                pst, pstk = gpl()
                                B.mm(pst[:, 0:T], pstk, [(constf[:, 832:960], osq[:, 0:T])], r=[osqk, "constf"])
                                rst, rstk = h.lf, h.lfk
                                B.op("act", lambda e: e.activation(rst[:, 0:T], pst[:, 0:T], AF.Sqrt, bias=EPS), r=[pstk], w=[rstk])
                                yield
                                B.op("dve", lambda e: e.reciprocal(rst[:, 0:T], rst[:, 0:T]), r=[rstk], w=[rstk])
                                yield
                                B.op("dve", lambda e: e.scalar_tensor_tensor(rst[:, 0:T], pob[:, 0:T], V(f"hng{o_}", hd), rst[:, 0:T], ALU.mult, ALU.mult),
                                     r=[pok, rstk, "vecs"], w=[rstk])
                                yield
                                B.op("dve", lambda e: e.tensor_tensor(yT[:, 4 + hd, :], rst[:, 0:T], h.sd[:, 0:T], ALU.mult), r=[rstk, h.sdk], w=[f"yT{b}_{4 + hd}"])
                                yield

                            grp = [([(lambda blk, hd=hd: head_chain(hd, blk)) for hd in range(4)], int(_os.environ.get("ODD_HB", "2")), 2),
                                   ([(lambda blk, c=c: pool_chain(c, blk)) for c in range(4)], int(_os.environ.get("ODD_PB", "2")))]
                            if C.next_normA is not None:
                                grp.append(([C.next_normA], 1))
                            run_chains2(grp)
                            tile_end(C, ti, T, NT, f"gmix{l}",
                                     (lambda ti=ti, yT=yT, b=b: out_proj_gen(C, src, dst, ti, T, yT, [f"yT{b}_{c}" for c in range(KC)], KC, wo, lambda hf: [f"wo{hf}"], 8)), 1)
                        tile_finish(C, NT)
                B.barrier()
            src = dst
        B.barrier(force=True)
    return nc, B.ninstr


_PROG_CACHE = {}


def _get_prog(S, layers, do_final):
    key = (S, tuple(layers), do_final)
    if key not in _PROG_CACHE:
        _PROG_CACHE[key] = build_program(S, list(layers), do_final)[0]
    return _PROG_CACHE[key]


def run_layers(xb, packed, layers, do_final, S):
    nc = _get_prog(S, layers, do_final)
    n = xb.shape[0]
    real = [0, 1, 4, 5][:n]
    zero = {k: np.zeros_like(v) for k, v in packed.items()}
    zero["x"] = np.zeros_like(xb[0])
    in_maps = []
    for i in range(8):
        if i in real:
            m = dict(packed)
            m["x"] = np.ascontiguousarray(xb[real.index(i)])
        else:
            m = zero
        in_maps.append(m)
    res = run_bass_kernel_spmd(nc, in_maps, core_ids=list(range(8)))
    return np.stack([res.results[i]["out"] for i in real])


def kernel(**inputs):
    packed = host_pack(inputs)
    x = np.asarray(inputs["x"], np.float32)
    out = run_layers(x, packed, [0, 1, 2, 3], True, SEQ)
    return out.astype(np.float32)
```
